# Optimizing a Trainium2 kernel written in Bass

```python
import jax, jax.numpy as jnp
from jax import lax
import numpy as np

D_MODEL = 2048
BATCH = 1
SEQ = 8192
DEPTH = 1

N_MEM = 256
EPS = 1e-5
D_MIX = D_MODEL
D_A = D_MIX // 2
D_B = D_MIX - D_A
CHUNK = 128
A_GROUPS = 8
A_GROUP_DIM = D_A // A_GROUPS
HEAD_DIM = 64
B_Q_HEADS = D_B // HEAD_DIM
B_KV_HEADS = 2
B_GROUP = B_Q_HEADS // B_KV_HEADS
WINDOW = 128
BLOCK = 128
X_HEADS = 4
X_HEAD_DIM = D_MODEL // X_HEADS
D_FF = 5632
IN_COLS = 2 * D_A + D_B + 2 * B_KV_HEADS * HEAD_DIM
NEG = -1e30

kernel_name = "hybrid_sgu_swa_sink_macaron_layer"


def rmsnorm(x, g):
    xf = x.astype(jnp.float32)
    y = xf * lax.rsqrt(jnp.mean(xf * xf, axis=-1, keepdims=True) + EPS)
    return (y * g.astype(jnp.float32)).astype(x.dtype)


def swiglu(x, w_gate, w_up, w_down):
    return (jax.nn.silu(x @ w_gate) * (x @ w_up)) @ w_down


def spatial_gating(z_uv, g_v, w_s, b_s):
    b, s, _ = z_uv.shape
    z = jax.nn.gelu(z_uv, approximate=False)
    u, v = z[..., :D_A], z[..., D_A:]
    v = rmsnorm(v, g_v)
    v = v.reshape(b, s // CHUNK, CHUNK, A_GROUPS, A_GROUP_DIM)
    causal = jnp.tril(jnp.ones((CHUNK, CHUNK), dtype=bool))
    ws = jnp.where(causal[None], w_s, jnp.zeros_like(w_s))
    sv = jnp.einsum('gts,bnsgc->bntgc', ws, v) + jnp.transpose(b_s)[None, None, :, :, None]
    return u * sv.reshape(b, s, D_A)


def window_attention_sinks(q, k, v, sinks):
    b, s, _ = q.shape
    nb = s // BLOCK
    q = q.reshape(b, nb, BLOCK, B_KV_HEADS, B_GROUP, HEAD_DIM)
    k = k.reshape(b, nb, BLOCK, B_KV_HEADS, HEAD_DIM)
    v = v.reshape(b, nb, BLOCK, B_KV_HEADS, HEAD_DIM)
    pad = ((0, 0), (1, 0), (0, 0), (0, 0), (0, 0))
    kk = jnp.concatenate([jnp.pad(k, pad)[:, :-1], k], axis=2)
    vv = jnp.concatenate([jnp.pad(v, pad)[:, :-1], v], axis=2)
    scores = jnp.einsum('bnqkgd,bnskd->bnkgqs', q, kk,
                        preferred_element_type=jnp.float32) * (HEAD_DIM ** -0.5)
    qpos = jnp.arange(BLOCK)[:, None] + BLOCK
    kpos = jnp.arange(2 * BLOCK)[None, :]
    diff = qpos - kpos
    band = (diff >= 0) & (diff < WINDOW)
    first = (jnp.arange(nb)[:, None, None] == 0) & (kpos[None] < BLOCK)
    mask = band[None] & ~first
    scores = jnp.where(mask[None, :, None, None], scores, NEG)
    sink = jnp.broadcast_to(
        sinks.astype(jnp.float32).reshape(1, 1, B_KV_HEADS, B_GROUP, 1, 1),
        scores.shape[:-1] + (1,))
    probs = jax.nn.softmax(jnp.concatenate([scores, sink], axis=-1), axis=-1)[..., :-1]
    out = jnp.einsum('bnkgqs,bnskd->bnqkgd', probs.astype(vv.dtype), vv)
    return out.reshape(b, s, D_B)


def cross_attention(hn, memn, w_q, w_kv, w_o):
    b, s, _ = hn.shape
    m = memn.shape[1]
    q = (hn @ w_q).reshape(b, s, X_HEADS, X_HEAD_DIM)
    kv = memn @ w_kv
    k = kv[..., :D_MODEL].reshape(b, m, X_HEADS, X_HEAD_DIM)
    v = kv[..., D_MODEL:].reshape(b, m, X_HEADS, X_HEAD_DIM)
    scores = jnp.einsum('bshd,bmhd->bhsm', q, k,
                        preferred_element_type=jnp.float32) * (X_HEAD_DIM ** -0.5)
    probs = jax.nn.softmax(scores, axis=-1).astype(v.dtype)
    out = jnp.einsum('bhsm,bmhd->bshd', probs, v).reshape(b, s, D_MODEL)
    return out @ w_o


def setup_inputs(seed: int = 0) -> dict:
    key = jax.random.key(seed)
    ks = jax.random.split(key, 32)
    L, D, F = DEPTH, D_MODEL, D_FF

    def nrm(k, shape, scale):
        return jax.random.normal(k, shape, dtype=jnp.float32) * scale

    def gain(k, shape):
        return 1.0 + 0.05 * jax.random.normal(k, shape, dtype=jnp.float32)

    return {
        "x": nrm(ks[0], (BATCH, SEQ, D), 1.0),
        "mem": nrm(ks[1], (BATCH, N_MEM, D), 1.0),
        "g_ffn1": gain(ks[2], (L, D)),
        "w1_gate": nrm(ks[3], (L, D, F), D ** -0.5),
        "w1_up": nrm(ks[4], (L, D, F), D ** -0.5),
        "w1_down": nrm(ks[5], (L, F, D), F ** -0.5),
        "g_mix": gain(ks[6], (L, D)),
        "w_in": nrm(ks[7], (L, D, IN_COLS), D ** -0.5),
        "g_v": gain(ks[8], (L, D_A)),
        "w_s": nrm(ks[9], (L, A_GROUPS, CHUNK, CHUNK), 0.5 * CHUNK ** -0.5),
        "b_s": 1.0 + 0.1 * jax.random.normal(ks[10], (L, A_GROUPS, CHUNK), dtype=jnp.float32),
        "sinks": nrm(ks[11], (L, B_Q_HEADS), 0.5),
        "g_a_out": gain(ks[12], (L, D_A)),
        "g_b_out": gain(ks[13], (L, D_B)),
        "w_out": nrm(ks[14], (L, D_MIX, D), D_MIX ** -0.5),
        "g_x": gain(ks[15], (L, D)),
        "g_mem": gain(ks[16], (L, D)),
        "w_xq": nrm(ks[17], (L, D, D), D ** -0.5),
        "w_xkv": nrm(ks[18], (L, D, 2 * D), D ** -0.5),
        "w_xo": nrm(ks[19], (L, D, D), D ** -0.5),
        "g_ffn2": gain(ks[20], (L, D)),
        "w2_gate": nrm(ks[21], (L, D, F), D ** -0.5),
        "w2_up": nrm(ks[22], (L, D, F), D ** -0.5),
        "w2_down": nrm(ks[23], (L, F, D), F ** -0.5),
        "g_final": gain(ks[24], (D,)),
    }


def reference(x, mem, g_ffn1, w1_gate, w1_up, w1_down, g_mix, w_in, g_v, w_s, b_s,
              sinks, g_a_out, g_b_out, w_out, g_x, g_mem, w_xq, w_xkv, w_xo,
              g_ffn2, w2_gate, w2_up, w2_down, g_final):
    o_q = 2 * D_A
    o_k = o_q + D_B
    o_v = o_k + B_KV_HEADS * HEAD_DIM
    h = x
    for l in range(DEPTH):
        h = h + 0.5 * swiglu(rmsnorm(h, g_ffn1[l]), w1_gate[l], w1_up[l], w1_down[l])
        z = rmsnorm(h, g_mix[l]) @ w_in[l]
        y_a = spatial_gating(z[..., :o_q], g_v[l], w_s[l], b_s[l])
        y_b = window_attention_sinks(z[..., o_q:o_k], z[..., o_k:o_v], z[..., o_v:], sinks[l])
        y = jnp.concatenate([rmsnorm(y_a, g_a_out[l]), rmsnorm(y_b, g_b_out[l])], axis=-1)
        h = h + y @ w_out[l]
        h = h + cross_attention(rmsnorm(h, g_x[l]), rmsnorm(mem, g_mem[l]),
                                w_xq[l], w_xkv[l], w_xo[l])
        h = h + 0.5 * swiglu(rmsnorm(h, g_ffn2[l]), w2_gate[l], w2_up[l], w2_down[l])
    return rmsnorm(h, g_final)
```

```python
import os
from contextlib import ExitStack

import numpy as np
import concourse.bass as bass
import concourse.mybir as mybir
from concourse.bass_utils import run_bass_kernel_spmd

F32 = mybir.dt.float32
BF16 = mybir.dt.bfloat16
U8 = mybir.dt.uint8
AF = mybir.ActivationFunctionType
ALU = mybir.AluOpType

NCORES = 8
D = 2048
SEQ = 8192
T = 1024
HALO = 128
TT = T + HALO
DFF = 5632
NFC = DFF // 128
NMEM = 256
EPS = 1e-5
NB = 4
GROUPS = [(0, 8), (8, 8), (16, 8), (24, 8), (32, 8), (40, 4)]

G_FFN1, G_MIX, G_X, G_MEM, G_FFN2, G_FIN, G_AOUT, G_BOUT, C_SINK = 0, 16, 32, 48, 64, 80, 96, 104, 112

DEBUG_PHASE = int(os.environ.get("MK_DEBUG_PHASE", "0"))


class Sched:
    ENGS = ("pe", "act", "dve", "pool", "sp")

    def __init__(self):
        self.ops = {e: [] for e in self.ENGS}
        self.ticks = {e: 0 for e in self.ENGS}
        self.last_w = {}
        self.readers = {}
        self.dma_counts = {}
        self.base = {}

    def _collect(self, eng, reads, writes, nobase):
        w = {}

        def add(t):
            if t is None:
                return
            k, v = t
            if w.get(k, 0) < v:
                w[k] = v

        if not nobase:
            for k, v in self.base.items():
                add((k, v))
        for k in reads:
            add(self.last_w.get(k))
        for k in writes:
            add(self.last_w.get(k))
            for rk, rv in self.readers.get(k, {}).items():
                add((rk, rv))
        if eng == "pe":
            w.pop("pe", None)
        return w

    def _register(self, tok, reads, writes):
        for k in reads:
            d = self.readers.setdefault(k, {})
            if d.get(tok[0], 0) < tok[1]:
                d[tok[0]] = tok[1]
        for k in writes:
            self.last_w[k] = tok
            self.readers[k] = {}

    def op(self, eng, fns, reads=(), writes=(), nobase=False):
        if not isinstance(fns, (list, tuple)):
            fns = [fns]
        self.ticks[eng] += 1
        tok = (eng, self.ticks[eng])
        waits = self._collect(eng, reads, writes, nobase)
        self.ops[eng].append((list(fns), waits, tok, None))
        self._register(tok, reads, writes)
        return tok

    def dma(self, eng, fn, semkey, reads=(), writes=(), nobase=False):
        self.dma_counts[semkey] = self.dma_counts.get(semkey, 0) + 16
        tok = (semkey, self.dma_counts[semkey])
        waits = self._collect("dma:" + semkey, reads, writes, nobase)
        self.ops[eng].append(([fn], waits, None, semkey))
        self._register(tok, reads, writes)
        return tok

    def barrier(self):
        for e in ("pe", "act", "dve", "pool"):
            if self.ticks[e]:
                self.base[e] = self.ticks[e]
        for k, v in self.dma_counts.items():
            if not k.startswith("ring"):
                self.base[k] = v

    def final_wait(self, eng, toks):
        w = {}
        for k, v in toks:
            w[k] = max(w.get(k, 0), v)
        self.ops[eng].append(([], w, None, None))

    def check(self):
        pc = {e: 0 for e in self.ENGS}
        sem = {}
        progress = True
        while progress:
            progress = False
            for e in self.ENGS:
                while pc[e] < len(self.ops[e]):
                    fns, waits, tok, dmakey = self.ops[e][pc[e]]
                    if any(sem.get(k, 0) < v for k, v in waits.items()):
                        break
                    if tok is not None:
                        sem[tok[0]] = sem.get(tok[0], 0) + 1
                        assert sem[tok[0]] == tok[1], (tok, sem[tok[0]])
                    if dmakey is not None:
                        sem[dmakey] = sem.get(dmakey, 0) + 16
                    pc[e] += 1
                    progress = True
        stuck = {e: (pc[e], len(self.ops[e])) for e in self.ENGS if pc[e] < len(self.ops[e])}
        if stuck:
            for e, (p, n) in stuck.items():
                fns, waits, tok, dmakey = self.ops[e][p]
                print("STUCK", e, p, n, {k: (v, sem.get(k, 0)) for k, v in waits.items() if sem.get(k, 0) < v})
            raise RuntimeError("deadlock in semaphore protocol")

    def emit(self, block, sems):
        needed = {e: set() for e in self.ENGS}
        for e in self.ENGS:
            for fns, waits, tok, dmakey in self.ops[e]:
                for k, v in waits.items():
                    if k in needed:
                        needed[k].add(v)
        remap = {e: {v: i + 1 for i, v in enumerate(sorted(needed[e]))} for e in self.ENGS}

        def run(engname, engine):
            waited = {}
            for fns, waits, tok, dmakey in self.ops[engname]:
                for k, v in waits.items():
                    if k in remap:
                        v = remap[k][v]
                    if waited.get(k, 0) < v:
                        engine.wait_ge(sems[k], v)
                        waited[k] = v
                inst = None
                for fn in fns:
                    inst = fn(engine)
                if tok is not None and tok[1] in remap[tok[0]]:
                    inst.then_inc(sems[tok[0]], 1)
                if dmakey is not None:
                    inst.then_inc(sems[dmakey], 16)

        @block.tensor
        def _(e):
            run("pe", e)

        @block.scalar
        def _(e):
            run("act", e)

        @block.vector
        def _(e):
            run("dve", e)

        @block.gpsimd
        def _(e):
            run("pool", e)

        @block.sync
        def _(e):
            run("sp", e)


def MM(out, lhsT, rhs, start, stop):
    return lambda e: e.matmul(out, lhsT=lhsT, rhs=rhs, start=start, stop=stop)


def ACTV(out, in_, func, bias=None, scale=1.0, accum_out=None):
    kw = {}
    if bias is not None:
        kw["bias"] = bias
    if accum_out is not None:
        kw["accum_out"] = accum_out
    return lambda e: e.activation(out=out, in_=in_, func=func, scale=scale, **kw)


def TTOP(out, in0, in1, op):
    return lambda e: e.tensor_tensor(out=out, in0=in0, in1=in1, op=op)


def STT(out, in0, scalar, in1, op0, op1):
    return lambda e: e.scalar_tensor_tensor(out=out, in0=in0, scalar=scalar, in1=in1, op0=op0, op1=op1)


def TS(out, in0, s1, s2, op0, op1=None):
    if op1 is None:
        return lambda e: e.tensor_scalar(out=out, in0=in0, scalar1=s1, scalar2=s2, op0=op0)
    return lambda e: e.tensor_scalar(out=out, in0=in0, scalar1=s1, scalar2=s2, op0=op0, op1=op1)


def RECIP(out, in_):
    return lambda e: e.reciprocal(out=out, in_=in_)


def COPY(out, in_):
    return lambda e: e.tensor_copy(out=out, in_=in_)


def MEMSET(ap, v):
    return lambda e: e.memset(ap, v)


def DMA(out, in_):
    return lambda e: e.dma_start(out=out, in_=in_)


def blks(lo, w):
    return range(lo // 128, (lo + w + 127) // 128)


def _col_tile(W, cols):
    sub = W[:, cols]
    return np.ascontiguousarray(sub.reshape(16, 128, 128).transpose(1, 0, 2)).reshape(128, 2048)


def _down_tile(W, j0, Gg, d0, DD):
    sub = W[j0 * 128:(j0 + Gg) * 128, d0:d0 + DD]
    return np.ascontiguousarray(sub.reshape(Gg, 128, DD).transpose(1, 0, 2)).reshape(128, 2048)


def tile_plan():
    plan = []

    def ffn(tag):
        for (j0, Gg) in GROUPS:
            for fi in range(Gg):
                plan.append((tag + "_gate", "col", (j0 + fi) * 128))
                plan.append((tag + "_up", "col", (j0 + fi) * 128))
            DD = 2048 // Gg
            for dq in range(Gg):
                plan.append((tag + "_down", "down", (j0, Gg, dq * DD, DD)))

    ffn("w1")
    plan.append(("w_in", "cols", ("kk", 0)))
    plan.append(("w_in", "cols", ("kk", 1)))
    plan.append(("w_in", "cols", ("vv", 0)))
    plan.append(("w_in", "cols", ("vv", 1)))
    for j in range(8):
        plan.append(("w_in", "col", 2048 + j * 128))
    for j in range(8):
        plan.append(("w_in", "col", 1024 + j * 128))
    for j in range(8):
        plan.append(("w_in", "col", j * 128))
    for j in range(16):
        plan.append(("w_xkv", "col", j * 128))
    for j in range(16):
        plan.append(("w_xkv", "col", 2048 + j * 128))
    for j in range(16):
        plan.append(("w_out", "col", j * 128))
    for j in range(16):
        plan.append(("w_xq", "col", j * 128))
    for j in range(16):
        plan.append(("w_xo", "col", j * 128))
    ffn("w2")
    return plan


def build_stream(inp):
    plan = tile_plan()
    mats = {
        "w1_gate": inp["w1_gate"][0], "w1_up": inp["w1_up"][0], "w1_down": inp["w1_down"][0],
        "w2_gate": inp["w2_gate"][0], "w2_up": inp["w2_up"][0], "w2_down": inp["w2_down"][0],
        "w_in": inp["w_in"][0], "w_out": inp["w_out"][0], "w_xq": inp["w_xq"][0],
        "w_xkv": inp["w_xkv"][0], "w_xo": inp["w_xo"][0],
    }
    out = np.empty((len(plan), 128, 2048), dtype=np.float32)
    for i, (name, kind, arg) in enumerate(plan):
        W = mats[name]
        if kind == "col":
            out[i] = _col_tile(W, np.arange(arg, arg + 128))
        elif kind == "down":
            out[i] = _down_tile(W, *arg)
        else:
            what, kv = arg
            base = 3072 if what == "kk" else 3200
            c = np.arange(base + kv * 64, base + kv * 64 + 64)
            out[i] = _col_tile(W, np.concatenate([c, c]))
    return out


def build_nc(ntiles):
    nc = bass.Bass("TRN2", target_bir_lowering=False)
    xT_d = nc.dram_tensor("xT", [128, 16, TT], F32, kind="ExternalInput").ap()
    memT_d = nc.dram_tensor("memT", [128, 16, NMEM], F32, kind="ExternalInput").ap()
    ws_d = nc.dram_tensor("wstream", [ntiles, 128, 2048], F32, kind="ExternalInput").ap()
    cstA_d = nc.dram_tensor("cstA", [128, 128], F32, kind="ExternalInput").ap()
    bbc_d = nc.dram_tensor("bbc", [128, 1024], F32, kind="ExternalInput").ap()
    gvbc_d = nc.dram_tensor("gvbc", [128, 1024], F32, kind="ExternalInput").ap()
    stg_d = nc.dram_tensor("stg", [128, 2560], F32, kind="ExternalInput").ap()
    outT_d = nc.dram_tensor("outT", [128, 16, T], F32, kind="ExternalOutput").ap()
    dbg_d = None
    if DEBUG_PHASE:
        dbg_d = nc.dram_tensor("dbg", [128, 16, TT], F32, kind="ExternalOutput").ap()

    S = Sched()
    with ExitStack() as st:
        ARENA = 212000
        arena = st.enter_context(nc.sbuf_tensor("arena", [128, ARENA], U8))
        base_addr = None
        for a in nc.allocations:
            if getattr(a, "name", "") == "arena_set":
                base_addr = a.memorylocations[0].addr
        assert base_addr is not None

        def at(name, shape, dt, off):
            esz = 4 if dt == F32 else 2
            n = 1
            for s_ in shape[1:]:
                n *= s_
            assert off + n * esz <= ARENA, (name, off, n * esz)
            assert off % 32 == 0, (name, off)
            return nc.alloc_sbuf_tensor_at(name, shape, dt, offset=base_addr + off)

        h = at("h", [128, 16, TT], F32, 0)
        RING0 = 73728
        ring = at("ring", [128, NB, 2048], BF16, RING0)
        C0 = RING0 + NB * 4096
        cstA = at("cstA_s", [128, 128], F32, C0)
        esink = at("esink", [128, 16], F32, C0 + 512)
        epsc = at("epsc", [128, 8], F32, C0 + 576)
        ones = at("ones", [128, 128], BF16, C0 + 608)
        masks = at("masks", [128, 4, 128], BF16, C0 + 864)
        wsTm = at("wsTm", [128, 8, 128], BF16, C0 + 1888)
        bbc = at("bbc_s", [128, 8, 128], F32, C0 + 3936)
        gvbc = at("gvbc_s", [128, 1024], F32, C0 + 8032)
        ssv = at("ssv", [128, 8], F32, C0 + 12128)
        rstdv = at("rstdv", [128, 8], F32, C0 + 12160)
        SC = C0 + 12800
        SCSZ = ARENA - SC
        NS = SC + 96256
        rstd = at("rstd", [128, TT], F32, NS)
        sq = at("sq", [128, 16, 256], BF16, NS + 4608)
        stg = at("stg_s", [128, 2560], F32, NS)
        sqB = at("sqB", [128, 16, 256], BF16, SC + 59392)
        xn = at("xn", [128, 16, TT], BF16, SC)
        act = at("act", [128, 8, TT], BF16, SC + 36864)
        stmp = at("stmp", [128, 2, 512], F32, SC + 55296)
        uT = at("uT", [128, 8, T], BF16, SC + 36864)
        vtok = at("vtok", [128, 8, 1024], BF16, SC + 53248)
        qT = at("qT", [128, 8, T], BF16, SC + 69632)
        kT2 = at("kT2", [128, 2, TT], BF16, SC + 86016)
        kO = at("kO", [128, 2, TT], BF16, NS)
        rstd_m = at("rstd_m", [128, 256], F32, SC + 61440)
        rd2 = at("rd2", [128, 2, 512], F32, SC + 62464)
        vtok2 = at("vtok2", [128, 9, 2, 128], BF16, SC + 90624)
        ycg = at("ycg", [128, 16, T], BF16, SC)
        PT = at("PT", [128, 2, 2, 512], BF16, SC + 32768)
        t1 = at("t1", [128, 2, 4, 128], F32, SC + 32768)
        rd = at("rd", [128, 512], F32, SC + 103424)
        sq2 = at("sq2", [128, 2, 512], BF16, SC + 103424)
        sqb = at("sqb", [128, 2, 2, 128], BF16, SC + 105472)
        ybf = at("ybf", [128, 2, 2, 128], F32, SC + 106496)
        junk = at("junk", [128, 1024], BF16, SC + 106496)
        wtmp = at("wtmp", [128, 2, 512], F32, SC + 69632)
        memT = at("memT_s", [128, 16, NMEM], F32, SC + 36864)
        memn = at("memn", [128, 16, NMEM], BF16, SC + 53248)
        kxT = at("kxT", [128, 16, NMEM], BF16, SC + 36864)
        vx = at("vx", [128, 2, 2048], BF16, SC + 45056)
        xn2 = at("xn2", [128, 16, T], BF16, SC)
        oxT = at("oxT", [128, 16, T], BF16, SC)
        PTx = at("PTx", [128, 2, 2, 512], BF16, SC + 32768)
        rdx = at("rdx", [128, 2, 512], F32, SC + 53248)
        qxT = at("qxT", [128, 16, T], BF16, SC + 57344)
        ostg = at("ostg", [128, 2, 4, T], F32, SC)
        sqF = at("sqF", [128, 16, T], BF16, SC + 36864)

        pb = [st.enter_context(nc.psum_tensor(f"pb{i}", [128, 512], F32)) for i in range(8)]

        semnames = list(Sched.ENGS) + [f"ring{i}" for i in range(NB)] + [
            "ld_x0", "ld_x1", "ld_x2", "ld_x3", "ld_x4", "ld_cA", "ld_bbc", "ld_gv", "ld_stg", "ld_mem", "st_out0", "st_out1", "st_dbg"]
        sems = {k: st.enter_context(nc.semaphore(k)) for k in semnames}
        block = st.enter_context(nc.Block())

        tile_idx = [0]

        def next_tile():
            i = tile_idx[0]
            tile_idx[0] += 1
            s = i % NB
            S.dma("pool", DMA(ring[:, s, :], ws_d[i]), f"ring{s}", writes=[("ring", s)], nobase=True)
            return s

        bank_rr = {"gu": 0, "dn": 0, "gen": 0}

        def hkeys(c, lo, w):
            return [("h", c, b) for b in blks(lo, w)]

        def rstd_from_bank(bank, w, dst_ap, Dn, wkeys, view4=False):
            src_ap = pb[bank][:, 0:w]
            if view4:
                src_ap = src_ap.rearrange("p (a b) -> p a b", a=4)
            S.op("act", ACTV(dst_ap, src_ap, AF.Ln, bias=epsc[:, 0:1], scale=1.0 / Dn),
                 reads=[("bank", bank), ("eps",)], writes=wkeys)
            S.op("act", ACTV(dst_ap, dst_ap, AF.Exp, scale=-0.5), reads=wkeys, writes=wkeys)

        norm_rr = [0]

        def norm_h(gcol0, lo, hi, dst, dst_name, dst_off):
            pieces = []
            p = lo
            while p < hi:
                w = min(256, hi - p)
                pieces.append((p, w))
                p += w
            sqbuf = [sq, sqB]

            def head(i):
                p, w = pieces[i]
                sb = sqbuf[i % 2]
                bk = 6 + i % 2
                S.op("act", ACTV(sb[:, :, 0:w], h[:, :, p:p + w], AF.Square),
                     reads=[k for c in range(16) for k in hkeys(c, p, w)], writes=[("sq", i % 2)])
                S.op("pe", [MM(pb[bk][:, 0:w], ones[:, :], sb[:, c, 0:w], c == 0, c == 15) for c in range(16)],
                     reads=[("sq", i % 2), ("ones",)], writes=[("bank", bk)])

            def tail(i):
                p, w = pieces[i]
                bk = 6 + i % 2
                rstd_from_bank(bk, w, rstd[:, p:p + w], D, [("rstd", b) for b in blks(p, w)])
                for c in range(16):
                    S.op("dve", STT(dst[:, c, p - dst_off:p - dst_off + w], h[:, c, p:p + w],
                                    cstA[:, gcol0 + c:gcol0 + c + 1], rstd[:, p:p + w], ALU.mult, ALU.mult),
                         reads=hkeys(c, p, w) + [("rstd", b) for b in blks(p, w)] + [("cstA",)],
                         writes=[(dst_name, c, b) for b in blks(p - dst_off, w)])

            for i in range(len(pieces)):
                head(i)
                if i >= 1:
                    tail(i - 1)
            tail(len(pieces) - 1)

        def run_pipeline(n, stages, lags):
            for step in range(n + max(lags)):
                for fn, lag in zip(stages, lags):
                    i = step - lag
                    if 0 <= i < n:
                        fn(i)


        def ffn(pieces):
            for (j0, Gg) in GROUPS:
                for fi in range(Gg):
                    sg = next_tile()
                    su = next_tile()
                    for (lo, w) in pieces:
                        k = bank_rr["gu"] % 2
                        bank_rr["gu"] += 1
                        bA, bB = 2 * k, 2 * k + 1
                        xr = [("xn", c, b) for c in range(16) for b in blks(lo, w)]
                        S.op("pe", [MM(pb[bA][:, 0:w], ring[:, sg, c * 128:(c + 1) * 128], xn[:, c, lo:lo + w],
                                       c == 0, c == 15) for c in range(16)],
                             reads=[("ring", sg)] + xr, writes=[("bank", bA)])
                        S.op("pe", [MM(pb[bB][:, 0:w], ring[:, su, c * 128:(c + 1) * 128], xn[:, c, lo:lo + w],
                                       c == 0, c == 15) for c in range(16)],
                             reads=[("ring", su)] + xr, writes=[("bank", bB)])
                        S.op("act", ACTV(stmp[:, k, 0:w], pb[bA][:, 0:w], AF.Silu),
                             reads=[("bank", bA)], writes=[("stmp", k)])
                        S.op("dve", TTOP(act[:, fi, lo:lo + w], stmp[:, k, 0:w], pb[bB][:, 0:w], ALU.mult),
                             reads=[("stmp", k), ("bank", bB)], writes=[("act", fi, b) for b in blks(lo, w)])
                DD = 2048 // Gg
                for dq in range(Gg):
                    sd = next_tile()
                    for dc in range(DD // 128):
                        d = dq * (DD // 128) + dc
                        for (lo, w) in pieces:
                            bC = 4 + bank_rr["dn"] % 2
                            bank_rr["dn"] += 1
                            S.op("pe", [MM(pb[bC][:, 0:w], ring[:, sd, fi * DD + dc * 128:fi * DD + dc * 128 + 128],
                                           act[:, fi, lo:lo + w], fi == 0, fi == Gg - 1) for fi in range(Gg)],
                                 reads=[("ring", sd)] + [("act", fi, b) for fi in range(Gg) for b in blks(lo, w)],
                                 writes=[("bank", bC)])
                            S.op("dve", STT(h[:, d, lo:lo + w], pb[bC][:, 0:w], 0.5, h[:, d, lo:lo + w],
                                            ALU.mult, ALU.add),
                                 reads=[("bank", bC)] + hkeys(d, lo, w), writes=hkeys(d, lo, w))

        def dump_h_and_stop():
            tk = S.dma("sp", DMA(dbg_d[:], h[:]), "st_dbg",
                       reads=[("h", c, b) for c in range(16) for b in range(9)])
            return tk

        S.dma("sp", DMA(cstA[:], cstA_d[:]), "ld_cA", writes=[("cstA",)])
        S.dma("sp", DMA(stg[:], stg_d[:]), "ld_stg", writes=[("stg",)])
        S.op("dve", MEMSET(epsc[:], EPS), writes=[("eps",)])
        S.op("dve", MEMSET(ones[:], 1.0), writes=[("ones",)])
        S.op("dve", MEMSET(ssv[:], 0.0), writes=[("ssv",)])
        S.op("dve", COPY(masks[:].rearrange("p a b -> p (a b)"), stg[:, 0:512]), reads=[("stg",)], writes=[("masks",)])
        S.op("dve", TTOP(wsTm[:].rearrange("p a b -> p (a b)"), stg[:, 512:1536], stg[:, 1536:2560], ALU.mult),
             reads=[("stg",)], writes=[("wsTm",)])
        S.op("act", ACTV(esink[:], cstA[:, C_SINK:C_SINK + 16], AF.Exp), reads=[("cstA",)], writes=[("esink",)])
        S.barrier()
        for q, (lo_, w_) in enumerate([(0, 256), (256, 256), (512, 256), (768, 256), (1024, 128)]):
            S.dma("sp", DMA(h[:, :, lo_:lo_ + w_], xT_d[:, :, lo_:lo_ + w_]), f"ld_x{q}",
                  writes=[("h", c, b) for c in range(16) for b in blks(lo_, w_)], nobase=True)
        S.dma("sp", DMA(bbc[:].rearrange("p a b -> p (a b)"), bbc_d[:]), "ld_bbc", writes=[("bbc",)], nobase=True)
        S.dma("sp", DMA(gvbc[:], gvbc_d[:]), "ld_gv", writes=[("gvbc",)], nobase=True)

        norm_h(G_FFN1, 0, TT, xn, "xn", 0)
        ffn([(0, 384), (384, 384), (768, 384)])
        if DEBUG_PHASE == 1:
            tk = dump_h_and_stop()
            S.final_wait("sp", [tk])
            S.emit(block, sems)
            return nc

        S.barrier()
        norm_h(G_MIX, 0, TT, xn, "xn", 0)
        S.barrier()
        own = [(128, 512), (640, 512)]
        all3 = [(0, 384), (384, 384), (768, 384)]
        evac_rr = [0]

        def evac_copy(out, in_, reads, writes):
            evac_rr[0] += 1
            if evac_rr[0] % 2:
                S.op("act", ACTV(out, in_, AF.Copy), reads=reads, writes=writes)
            else:
                S.op("dve", COPY(out, in_), reads=reads, writes=writes)

        def gen_bank():
            b = bank_rr["gen"] % 4
            bank_rr["gen"] += 1
            return b

        def fm_tile(s, pieces, evac):
            for (lo, w) in pieces:
                b = gen_bank()
                S.op("pe", [MM(pb[b][:, 0:w], ring[:, s, c * 128:(c + 1) * 128], xn[:, c, lo:lo + w], c == 0, c == 15)
                            for c in range(16)],
                     reads=[("ring", s)] + [("xn", c, bb) for c in range(16) for bb in blks(lo, w)],
                     writes=[("bank", b)])
                evac(b, lo, w)

        def tm_tile(s, blocks, evac):
            for i0 in range(0, len(blocks), 4):
                grp = blocks[i0:i0 + 4]
                b = gen_bank()
                for qi, blk in enumerate(grp):
                    S.op("pe", [MM(pb[b][:, qi * 128:(qi + 1) * 128], xn[:, c, blk * 128:(blk + 1) * 128],
                                   ring[:, s, c * 128:(c + 1) * 128], c == 0, c == 15) for c in range(16)],
                         reads=[("ring", s)] + [("xn", c, blk) for c in range(16)], writes=[("bank", b)])
                evac(b, grp)

        S.op("dve", MEMSET(kT2[64:128, :, :], 0.0), writes=[("kT2z",)])
        S.op("dve", MEMSET(kO[0:64, :, :], 0.0), writes=[("kOz",)])

        def k_evac(b, lo, w, kv):
            S.op("act", ACTV(kT2[0:64, kv, lo:lo + w], pb[b][0:64, 0:w], AF.Copy),
                 reads=[("bank", b)], writes=[("kT2", kv, bb) for bb in blks(lo, w)])
            S.op("dve", COPY(kO[64:128, kv, lo:lo + w], pb[b][64:128, 0:w]),
                 reads=[("bank", b)], writes=[("kO", kv, bb) for bb in blks(lo, w)])

        for kv in range(2):
            s = next_tile()
            fm_tile(s, all3, lambda b, lo, w, kv=kv: k_evac(b, lo, w, kv))
        for kv in range(2):
            s = next_tile()
            tm_tile(s, list(range(9)), lambda b, grp, kv=kv: evac_copy(
                vtok2[:, grp[0]:grp[0] + len(grp), kv, :],
                pb[b][:, 0:128 * len(grp)].rearrange("p (a b) -> p a b", a=len(grp)),
                [("bank", b)], [("vtok2", blk, kv) for blk in grp]))
        for j in range(8):
            s = next_tile()
            fm_tile(s, own, lambda b, lo, w, j=j: evac_copy(
                qT[:, j, lo - 128:lo - 128 + w], pb[b][:, 0:w], [("bank", b)],
                [("qT", j, bb) for bb in blks(lo - 128, w)]))
        for j in range(8):
            s = next_tile()
            tm_tile(s, list(range(1, 9)), lambda b, grp, j=j: S.op(
                "act", ACTV(vtok[:, grp[0] - 1:grp[0] - 1 + len(grp), j * 128:(j + 1) * 128],
                            pb[b][:, 0:128 * len(grp)].rearrange("p (a b) -> p a b", a=len(grp)), AF.Gelu),
                reads=[("bank", b)], writes=[("vtok", blk - 1) for blk in grp]))
        for blk in range(8):
            S.op("act", ACTV(junk[:], vtok[:, blk, :], AF.Square, accum_out=ssv[:, blk:blk + 1]),
                 reads=[("vtok", blk), ("ssv",)], writes=[("junk",), ("ssvb", blk)])
        S.op("act", ACTV(rstdv[:], ssv[:], AF.Ln, bias=epsc[:, 0:1], scale=1.0 / 1024),
             reads=[("ssvb", b) for b in range(8)] + [("eps",)], writes=[("rstdv",)])
        S.op("act", ACTV(rstdv[:], rstdv[:], AF.Exp, scale=-0.5), reads=[("rstdv",)], writes=[("rstdv",)])
        for blk in range(8):
            S.op("dve", STT(vtok[:, blk, :], vtok[:, blk, :], rstdv[:, blk:blk + 1], gvbc[:, :], ALU.mult, ALU.mult),
                 reads=[("vtok", blk), ("rstdv",), ("gvbc",), ("junk",)], writes=[("vtok", blk)])
        for j in range(8):
            s = next_tile()
            fm_tile(s, own, lambda b, lo, w, j=j: S.op(
                "act", ACTV(uT[:, j, lo - 128:lo - 128 + w], pb[b][:, 0:w], AF.Gelu),
                reads=[("bank", b)], writes=[("uT", j, bb) for bb in blks(lo - 128, w)]))

        S.barrier()
        sg_iters = [(blk, half) for blk in range(8) for half in range(2)]
        sg_bank = {}

        def sg_A(i):
            blk, half = sg_iters[i]
            g0 = 4 * half
            b = i % 4
            sg_bank[i] = b
            S.op("pe", [MM(pb[b][:, gg * 128:(gg + 1) * 128], vtok[:, blk, (g0 + gg) * 128:(g0 + gg + 1) * 128],
                           wsTm[:, g0 + gg, :], True, True) for gg in range(4)],
                 reads=[("vtok", blk), ("wsTm",)], writes=[("bank", b)])

        def sg_B(i):
            blk, half = sg_iters[i]
            g0 = 4 * half
            kk = i % 2
            b = sg_bank[i]
            S.op("dve", TTOP(t1[:, kk, :, :], pb[b][:, :].rearrange("p (a b) -> p a b", a=4),
                             bbc[:, g0:g0 + 4, :], ALU.add),
                 reads=[("bank", b), ("bbc",)], writes=[("t1", kk)])
            S.op("dve", TTOP(t1[:, kk, :, :], t1[:, kk, :, :], uT[:, g0:g0 + 4, blk * 128:(blk + 1) * 128], ALU.mult),
                 reads=[("t1", kk)] + [("uT", g0 + gg, blk) for gg in range(4)], writes=[("t1", kk)])
            S.op("pool", TTOP(ycg[:, g0:g0 + 4, blk * 128:(blk + 1) * 128], t1[:, kk, :, :],
                              cstA[:, G_AOUT + g0:G_AOUT + g0 + 4].unsqueeze(2).to_broadcast([128, 4, 128]), ALU.mult),
                 reads=[("t1", kk), ("cstA",)], writes=[("ycg", g0 + gg, blk) for gg in range(4)])
            S.op("act", ACTV(sq2[:, kk, :], t1[:, kk, :, :].rearrange("p a b -> p (a b)"), AF.Square),
                 reads=[("t1", kk)], writes=[("sq2", kk)])

        def sg_C(i):
            blk, half = sg_iters[i]
            kk = i % 2
            bss = 4 + blk // 4
            S.op("pe", [MM(pb[bss][:, (blk % 4) * 128:(blk % 4 + 1) * 128], ones[:, :],
                           sq2[:, kk, gg * 128:(gg + 1) * 128],
                           half == 0 and gg == 0, half == 1 and gg == 3) for gg in range(4)],
                 reads=[("sq2", kk), ("ones",)], writes=[("bank", bss)])

        run_pipeline(16, [sg_C, sg_A, sg_B], [3, 0, 1])
        for half in range(2):
            rstd_from_bank(4 + half, 512, h[:, 4 * half:4 * half + 4, 0:128], 1024,
                           [("h", 4 * half + q, 0) for q in range(4)], view4=True)

        S.barrier()
        S.dma("sp", DMA(memT[:], memT_d[:]), "ld_mem", writes=[("memT",)])
        S.op("act", ACTV(sq[:, :, 0:256], memT[:, :, :], AF.Square), reads=[("memT",)], writes=[("sq", 0)])
        S.op("pe", [MM(pb[6][:, 0:256], ones[:, :], sq[:, c, 0:256], c == 0, c == 15) for c in range(16)],
             reads=[("sq", 0), ("ones",)], writes=[("bank", 6)])
        rstd_from_bank(6, 256, rstd_m[:, :], D, [("rstd_m",)])
        for c in range(16):
            S.op("dve", STT(memn[:, c, :], memT[:, c, :], cstA[:, G_MEM + c:G_MEM + c + 1], rstd_m[:, :],
                            ALU.mult, ALU.mult),
                 reads=[("memT",), ("rstd_m",), ("cstA",)], writes=[("memn", c)])
        S.barrier()

        def kvx_tile(i):
            s = next_tile()
            if i < 16:
                j = i
                S.op("pe", [MM(pb[7][:, 0:256], ring[:, s, c * 128:(c + 1) * 128], memn[:, c, :], c == 0, c == 15)
                            for c in range(16)],
                     reads=[("ring", s)] + [("memn", c) for c in range(16)], writes=[("bank", 7)])
                evac_copy(kxT[:, j, :], pb[7][:, 0:256], [("bank", 7)], [("kxT", j)])
            else:
                j = i - 16
                for mb in range(2):
                    S.op("pe", [MM(pb[7][:, mb * 128:(mb + 1) * 128], memn[:, c, mb * 128:(mb + 1) * 128],
                                   ring[:, s, c * 128:(c + 1) * 128], c == 0, c == 15) for c in range(16)],
                         reads=[("ring", s)] + [("memn", c) for c in range(16)], writes=[("bank", 7)])
                evac_copy(vx[:, :, j * 128:(j + 1) * 128], pb[7][:, 0:256].rearrange("p (a b) -> p a b", a=2),
                          [("bank", 7)], [("vx", j)])

        att_iters = [(n, hg) for n in range(8) for hg in range(4)]

        def att_A1(i):
            n, hg = att_iters[i]
            kvh = hg // 2
            kk = i % 2
            for kb in range(2):
                bS = kb
                kblk = n + kb
                fns = []
                for hh in range(4):
                    hd = 4 * hg + hh
                    j, half = hd // 2, hd % 2
                    ksrc = kT2 if half == 0 else kO
                    fns.append(MM(pb[bS][:, hh * 128:(hh + 1) * 128],
                                  ksrc[:, kvh, kblk * 128:(kblk + 1) * 128],
                                  qT[:, j, n * 128:(n + 1) * 128], True, True))
                S.op("pe", fns,
                     reads=[("kT2", kvh, kblk), ("kO", kvh, kblk), ("kT2z",), ("kOz",)] +
                           [("qT", (4 * hg + hh) // 2, n) for hh in range(4)],
                     writes=[("bank", bS)])
                S.op("act", ACTV(PT[:, kk, kb, :], pb[bS][:, :], AF.Exp, scale=0.125),
                     reads=[("bank", bS)], writes=[("PT", kk, kb)])

        def att_A2(i):
            n, hg = att_iters[i]
            kk = i % 2
            m0 = 2 if n == 0 else 0
            S.op("dve", TTOP(PT[:, kk, :, :].rearrange("p k (h q) -> p k h q", h=4),
                             PT[:, kk, :, :].rearrange("p k (h q) -> p k h q", h=4),
                             masks[:, m0:m0 + 2, :].unsqueeze(2).to_broadcast([128, 2, 4, 128]), ALU.mult),
                 reads=[("PT", kk, 0), ("PT", kk, 1), ("masks",)], writes=[("PT", kk, 0), ("PT", kk, 1)])

        def att_B1(i):
            n, hg = att_iters[i]
            kvh = hg // 2
            kk = i % 2
            bD, bO = 2 + 2 * kk, 3 + 2 * kk
            S.op("pe", [MM(pb[bD][:, :], ones[:, :], PT[:, kk, 0, :], True, False),
                        MM(pb[bD][:, :], ones[:, :], PT[:, kk, 1, :], False, True)],
                 reads=[("PT", kk, 0), ("PT", kk, 1), ("ones",)], writes=[("bank", bD)])
            S.op("pe", [MM(pb[bO][:, :], vtok2[:, n, kvh, :], PT[:, kk, 0, :], True, False),
                        MM(pb[bO][:, :], vtok2[:, n + 1, kvh, :], PT[:, kk, 1, :], False, True)],
                 reads=[("PT", kk, 0), ("PT", kk, 1), ("vtok2", n, kvh), ("vtok2", n + 1, kvh)],
                 writes=[("bank", bO)])
            S.op("dve", TTOP(rd2[:, kk, :].rearrange("p (a b) -> p a b", a=4),
                             pb[bD][:, :].rearrange("p (a b) -> p a b", a=4),
                             esink[:, 4 * hg:4 * hg + 4].unsqueeze(2).to_broadcast([128, 4, 128]), ALU.add),
                 reads=[("bank", bD), ("esink",)], writes=[("rd", kk)])

        def att_B2(i):
            kk = i % 2
            S.op("act", ACTV(rd2[:, kk, :], rd2[:, kk, :], AF.Ln), reads=[("rd", kk)], writes=[("rd", kk)])
            S.op("act", ACTV(rd2[:, kk, :], rd2[:, kk, :], AF.Exp, scale=-1.0), reads=[("rd", kk)],
                 writes=[("rd", kk)])

        def att_B3(i):
            n, hg = att_iters[i]
            kk = i % 2
            bO = 3 + 2 * kk
            for half in range(2):
                p0 = half * 64
                S.op("dve", TTOP(ybf[p0:p0 + 64, kk, :, :],
                                 pb[bO][p0:p0 + 64, :].rearrange("p (a b c) -> p a b c", a=2, b=2)[:, :, half, :],
                                 rd2[p0:p0 + 64, kk, :].rearrange("p (a b c) -> p a b c", a=2, b=2)[:, :, half, :],
                                 ALU.mult),
                     reads=[("bank", bO), ("rd", kk)], writes=[("ybf", kk, half)])
            S.op("dve", TTOP(ycg[:, 8 + 2 * hg:8 + 2 * hg + 2, n * 128:(n + 1) * 128], ybf[:, kk, :, :],
                             cstA[:, G_BOUT + 2 * hg:G_BOUT + 2 * hg + 2].unsqueeze(2).to_broadcast([128, 2, 128]),
                             ALU.mult),
                 reads=[("ybf", kk, 0), ("ybf", kk, 1), ("cstA",)],
                 writes=[("ycg", 8 + 2 * hg, n), ("ycg", 8 + 2 * hg + 1, n)])

        def att_B4(i):
            kk = i % 2
            S.op("act", ACTV(sqb[:, kk, :, :], ybf[:, kk, :, :], AF.Square),
                 reads=[("ybf", kk, 0), ("ybf", kk, 1)], writes=[("sqb", kk)])

        def att_C(i):
            n, hg = att_iters[i]
            kk = i % 2
            bss = 6
            S.op("pe", [MM(pb[bss][:, (n % 4) * 128:(n % 4 + 1) * 128], ones[:, :], sqb[:, kk, i2, :],
                           hg == 0 and i2 == 0, hg == 3 and i2 == 1) for i2 in range(2)],
                 reads=[("sqb", kk), ("ones",)], writes=[("bank", bss)])
            if i % 16 == 15:
                hb = i // 16
                rstd_from_bank(6, 512, h[:, 8 + 4 * hb:8 + 4 * hb + 4, 0:128], 1024,
                               [("h", 8 + 4 * hb + q, 0) for q in range(4)], view4=True)

        run_pipeline(32, [att_C, att_B1, att_A1, att_B3, att_B2, att_A2, att_B4, kvx_tile], [4, 1, 0, 2, 1, 0, 2, 0])

        S.barrier()
        for j in range(16):
            s = next_tile()
            for pi, (lo, w) in enumerate(own):
                lo0 = lo - 128
                bA, bB = gen_bank(), gen_bank()
                for part, bnk in ((0, bA), (1, bB)):
                    S.op("pe", [MM(pb[bnk][:, 0:w], ring[:, s, (8 * part + c) * 128:(8 * part + c + 1) * 128],
                                   ycg[:, 8 * part + c, lo0:lo0 + w], c == 0, c == 7) for c in range(8)],
                         reads=[("ring", s)] + [("ycg", 8 * part + c, bb) for c in range(8) for bb in blks(lo0, w)],
                         writes=[("bank", bnk)])
                b0 = lo0 // 128
                S.op("dve", TTOP(wtmp[:, 0, :].rearrange("p (a b) -> p a b", a=4),
                                 pb[bA][:, :].rearrange("p (a b) -> p a b", a=4), h[:, b0:b0 + 4, 0:128], ALU.mult),
                     reads=[("bank", bA)] + [("h", b0 + q, 0) for q in range(4)], writes=[("wtmp", 0)])
                S.op("dve", TTOP(wtmp[:, 1, :].rearrange("p (a b) -> p a b", a=4),
                                 pb[bB][:, :].rearrange("p (a b) -> p a b", a=4), h[:, 8 + b0:8 + b0 + 4, 0:128],
                                 ALU.mult),
                     reads=[("bank", bB)] + [("h", 8 + b0 + q, 0) for q in range(4)], writes=[("wtmp", 1)])
                S.op("dve", TTOP(wtmp[:, 0, :], wtmp[:, 0, :], wtmp[:, 1, :], ALU.add),
                     reads=[("wtmp", 0), ("wtmp", 1)], writes=[("wtmp", 0)])
                S.op("dve", TTOP(h[:, j, lo:lo + w], h[:, j, lo:lo + w], wtmp[:, 0, :], ALU.add),
                     reads=[("wtmp", 0)] + hkeys(j, lo, w), writes=hkeys(j, lo, w))
        if DEBUG_PHASE == 2:
            tk = dump_h_and_stop()
            S.final_wait("sp", [tk])
            S.emit(block, sems)
            return nc

        S.barrier()
        for c in range(16):
            S.op("dve", TS(xn2[:, c, :], h[:, c, 128:TT], cstA[:, G_X + c:G_X + c + 1], None, ALU.mult),
                 reads=hkeys(c, 128, T) + [("cstA",)], writes=[("xn2", c, bb) for bb in range(8)])
        def x_stat(pi):
            p, w = 128 + 256 * pi, 256
            bk = 6 + pi % 2
            S.op("act", ACTV(sq[:, :, 0:w], h[:, :, p:p + w], AF.Square),
                 reads=[k for c in range(16) for k in hkeys(c, p, w)], writes=[("sq", 0)])
            S.op("pe", [MM(pb[bk][:, 0:w], ones[:, :], sq[:, c, 0:w], c == 0, c == 15) for c in range(16)],
                 reads=[("sq", 0), ("ones",)], writes=[("bank", bk)])
            rstd_from_bank(bk, w, rstd[:, p:p + w], D, [("rstd", b) for b in blks(p, w)])

        own0 = [(0, 512), (512, 512)]
        xq_slot = {}

        xq_bank = {}

        def xq_mm(j, pi):
            if j not in xq_slot:
                xq_slot[j] = next_tile()
            s = xq_slot[j]
            lo, w = own0[pi]
            b = gen_bank()
            xq_bank[(j, pi)] = b
            S.op("pe", [MM(pb[b][:, 0:w], ring[:, s, c * 128:(c + 1) * 128], xn2[:, c, lo:lo + w], c == 0, c == 15)
                        for c in range(16)],
                 reads=[("ring", s)] + [("xn2", c, bb) for c in range(16) for bb in blks(lo, w)],
                 writes=[("bank", b)])

        def xq_ev(j, pi):
            lo, w = own0[pi]
            b = xq_bank[(j, pi)]
            S.op("dve", TTOP(qxT[:, j, lo:lo + w], pb[b][:, 0:w], rstd[:, 128 + lo:128 + lo + w], ALU.mult),
                 reads=[("bank", b)] + [("rstd", bb) for bb in blks(128 + lo, w)],
                 writes=[("qxT", j, bb) for bb in blks(lo, w)])

        x_stat(0)
        xq_mm(0, 0)
        x_stat(1)
        xq_ev(0, 0)
        xq_mm(1, 0)
        x_stat(2)
        xq_ev(1, 0)
        xq_mm(0, 1)
        x_stat(3)
        xq_ev(0, 1)
        xq_mm(1, 1)
        xq_ev(1, 1)
        for j in range(2, 16):
            for pi in range(2):
                xq_mm(j, pi)
                xq_ev(j, pi)
        S.barrier()
        xscale = float(512 ** -0.5)
        x_iters = [(hx, lo, w) for hx in range(4) for (lo, w) in own0]
        obank = [0]

        def x_A(i):
            hx, lo, w = x_iters[i]
            kk = i % 2
            for mb in range(2):
                b = 2 * kk + mb
                S.op("pe", [MM(pb[b][:, :], kxT[:, hx * 4 + dc, mb * 128:(mb + 1) * 128],
                               qxT[:, hx * 4 + dc, lo:lo + w], dc == 0, dc == 3) for dc in range(4)],
                     reads=[("kxT", hx * 4 + dc) for dc in range(4)] +
                           [("qxT", hx * 4 + dc, bb) for dc in range(4) for bb in blks(lo, w)],
                     writes=[("bank", b)])
                S.op("act", ACTV(PTx[:, kk, mb, :], pb[b][:, :], AF.Exp, scale=xscale),
                     reads=[("bank", b)], writes=[("PTx", kk, mb)])

        def x_B1(i):
            hx, lo, w = x_iters[i]
            kk = i % 2
            S.op("pe", [MM(pb[4][:, :], ones[:, :], PTx[:, kk, 0, :], True, False),
                        MM(pb[4][:, :], ones[:, :], PTx[:, kk, 1, :], False, True)],
                 reads=[("PTx", kk, 0), ("PTx", kk, 1), ("ones",)], writes=[("bank", 4)])
            S.op("act", ACTV(rdx[:, kk, :], pb[4][:, :], AF.Ln), reads=[("bank", 4)], writes=[("rdx", kk)])
            S.op("act", ACTV(rdx[:, kk, :], rdx[:, kk, :], AF.Exp, scale=-1.0), reads=[("rdx", kk)],
                 writes=[("rdx", kk)])

        def x_B2(i):
            hx, lo, w = x_iters[i]
            kk = i % 2
            for dc in range(4):
                bo = 5 + obank[0] % 3
                obank[0] += 1
                S.op("pe", [MM(pb[bo][:, :], vx[:, 0, (hx * 4 + dc) * 128:(hx * 4 + dc + 1) * 128],
                               PTx[:, kk, 0, :], True, False),
                            MM(pb[bo][:, :], vx[:, 1, (hx * 4 + dc) * 128:(hx * 4 + dc + 1) * 128],
                               PTx[:, kk, 1, :], False, True)],
                     reads=[("PTx", kk, 0), ("PTx", kk, 1), ("vx", hx * 4 + dc)], writes=[("bank", bo)])
                S.op("dve", TTOP(oxT[:, hx * 4 + dc, lo:lo + w], pb[bo][:, :], rdx[:, kk, :], ALU.mult),
                     reads=[("bank", bo), ("rdx", kk)], writes=[("oxT", hx * 4 + dc, bb) for bb in blks(lo, w)])

        run_pipeline(8, [x_B2, x_B1, x_A], [2, 1, 0])

        for j in range(16):
            s = next_tile()
            for (lo, w) in own0:
                b = gen_bank()
                S.op("pe", [MM(pb[b][:, 0:w], ring[:, s, c * 128:(c + 1) * 128], oxT[:, c, lo:lo + w], c == 0, c == 15)
                            for c in range(16)],
                     reads=[("ring", s)] + [("oxT", c, bb) for c in range(16) for bb in blks(lo, w)],
                     writes=[("bank", b)])
                S.op("dve", TTOP(h[:, j, 128 + lo:128 + lo + w], h[:, j, 128 + lo:128 + lo + w], pb[b][:, 0:w], ALU.add),
                     reads=[("bank", b)] + hkeys(j, 128 + lo, w), writes=hkeys(j, 128 + lo, w))
        if DEBUG_PHASE == 3:
            if os.environ.get("MK_VERBOSE"):
                print("ticks@3", S.ticks)
            tk = dump_h_and_stop()
            S.final_wait("sp", [tk])
            S.emit(block, sems)
            return nc

        S.barrier()
        norm_h(G_FFN2, 128, TT, xn, "xn", 0)
        ffn(own)

        if DEBUG_PHASE == 4:
            tk = dump_h_and_stop()
            S.final_wait("sp", [tk])
            S.emit(block, sems)
            return nc

        S.barrier()
        toks = []
        for c in range(16):
            if c % 2 == 0:
                S.op("act", ACTV(sqF[:, c, :], h[:, c, 128:TT], AF.Square), reads=hkeys(c, 128, T), writes=[("sqF", c)])
            else:
                S.op("dve", TTOP(sqF[:, c, :], h[:, c, 128:TT], h[:, c, 128:TT], ALU.mult),
                     reads=hkeys(c, 128, T), writes=[("sqF", c)])
        for half in range(2):
            S.op("pe", [MM(pb[6 + half][:, :], ones[:, :], sqF[:, c, half * 512:(half + 1) * 512], c == 0, c == 15)
                        for c in range(16)],
                 reads=[("sqF", c) for c in range(16)] + [("ones",)], writes=[("bank", 6 + half)])
            rstd_from_bank(6 + half, 512, rstd[:, 128 + half * 512:128 + (half + 1) * 512], D,
                           [("rstd", b) for b in blks(128 + half * 512, 512)])
        for q in range(4):
            kk = q % 2
            for ci in range(4):
                c = 4 * q + ci
                S.op("dve", STT(ostg[:, kk, ci, :], h[:, c, 128:TT], cstA[:, G_FIN + c:G_FIN + c + 1],
                                rstd[:, 128:TT], ALU.mult, ALU.mult),
                     reads=hkeys(c, 128, T) + [("rstd", b) for b in blks(128, T)] + [("cstA",)],
                     writes=[("ostg", kk, ci)])
            toks.append(S.dma("sp", DMA(outT_d[:, 4 * q:4 * q + 4, :], ostg[:, kk, :, :]), f"st_out{kk}",
                              reads=[("ostg", kk, ci) for ci in range(4)]))
        S.final_wait("sp", toks[-2:])
        if os.environ.get("MK_VERBOSE"):
            print("ticks", S.ticks, "dma", S.dma_counts)
            S.check()
        S.emit(block, sems)
    return nc


def _fm(v):
    return np.ascontiguousarray(np.asarray(v, np.float32).reshape(-1, 128).T)


def prep_inputs(inp):
    x = np.asarray(inp["x"], np.float32)[0]
    mem = np.asarray(inp["mem"], np.float32)[0]
    wstream = build_stream({k: np.asarray(v, np.float32) for k, v in inp.items()})
    xp = np.concatenate([np.zeros((HALO, D), np.float32), x], axis=0)
    memT = np.ascontiguousarray(mem.reshape(NMEM, 16, 128).transpose(2, 1, 0))
    cstA = np.zeros((128, 128), np.float32)
    cstA[:, G_FFN1:G_FFN1 + 16] = _fm(inp["g_ffn1"][0])
    cstA[:, G_MIX:G_MIX + 16] = _fm(inp["g_mix"][0])
    cstA[:, G_X:G_X + 16] = _fm(inp["g_x"][0])
    cstA[:, G_MEM:G_MEM + 16] = _fm(inp["g_mem"][0])
    cstA[:, G_FFN2:G_FFN2 + 16] = _fm(inp["g_ffn2"][0])
    cstA[:, G_FIN:G_FIN + 16] = _fm(inp["g_final"])
    cstA[:, G_AOUT:G_AOUT + 8] = _fm(inp["g_a_out"][0])
    cstA[:, G_BOUT:G_BOUT + 8] = _fm(inp["g_b_out"][0])
    cstA[:, C_SINK:C_SINK + 16] = np.broadcast_to(np.asarray(inp["sinks"], np.float32)[0][None, :], (128, 16))
    bbc = np.ascontiguousarray(np.broadcast_to(np.asarray(inp["b_s"], np.float32)[0].reshape(1, 1024), (128, 1024)))
    gvbc = np.ascontiguousarray(np.broadcast_to(np.asarray(inp["g_v"], np.float32)[0].reshape(1, 1024), (128, 1024)))
    si = np.arange(128)[:, None]
    qi = np.arange(128)[None, :]
    one, zero = np.float32(1), np.float32(0)
    maskC = np.where(si <= qi, one, zero).astype(np.float32)
    maskP = np.where(si > qi, one, zero).astype(np.float32)
    wsT = np.ascontiguousarray(np.asarray(inp["w_s"], np.float32)[0].transpose(2, 0, 1)).reshape(128, 1024)
    tril = np.where(si <= qi, np.float32(1), np.float32(0)).astype(np.float32)
    tril8 = np.ascontiguousarray(np.broadcast_to(tril[:, None, :], (128, 8, 128))).reshape(128, 1024)
    in_maps = []
    for i in range(NCORES):
        xs = xp[i * T:i * T + TT]
        xT = np.ascontiguousarray(xs.reshape(TT, 16, 128).transpose(2, 1, 0))
        maskH = maskP if i > 0 else np.zeros((128, 128), np.float32)
        stg = np.ascontiguousarray(np.concatenate([maskP, maskC, maskH, maskC, wsT, tril8], axis=1))
        in_maps.append({"xT": xT, "memT": memT, "wstream": wstream, "cstA": cstA, "bbc": bbc, "gvbc": gvbc,
                        "stg": stg})
    return in_maps, wstream.shape[0]


def kernel(**inputs):
    in_maps, ntiles = prep_inputs(inputs)
    nc = build_nc(ntiles)
    res = run_bass_kernel_spmd(nc, in_maps, core_ids=list(range(NCORES)))
    if DEBUG_PHASE:
        return res
    outs = []
    for r in res.results:
        oT = np.asarray(r["outT"])
        outs.append(oT.transpose(2, 1, 0).reshape(T, D))
    return np.concatenate(outs, axis=0).reshape(1, SEQ, D).astype(np.float32)
```

```python
import os
from contextlib import ExitStack

import numpy as np
import concourse.bass as bass
import concourse.mybir as mybir
from concourse.bass_utils import run_bass_kernel_spmd

F32 = mybir.dt.float32
BF16 = mybir.dt.bfloat16
U8 = mybir.dt.uint8
AF = mybir.ActivationFunctionType
ALU = mybir.AluOpType

NCORES = 8
D = 2048
SEQ = 8192
T = 1024
HALO = 128
TT = T + HALO
DFF = 5632
NFC = DFF // 128
NMEM = 256
EPS = 1e-5
NB = 4
GROUPS = [(0, 8), (8, 8), (16, 8), (24, 8), (32, 8), (40, 4)]

G_FFN1, G_MIX, G_X, G_MEM, G_FFN2, G_FIN, G_AOUT, G_BOUT, C_SINK = 0, 16, 32, 48, 64, 80, 96, 104, 112

DEBUG_PHASE = int(os.environ.get("MK_DEBUG_PHASE", "0"))


class Sched:
    ENGS = ("pe", "act", "dve", "pool", "sp")

    def __init__(self):
        self.ops = {e: [] for e in self.ENGS}
        self.ticks = {e: 0 for e in self.ENGS}
        self.last_w = {}
        self.readers = {}
        self.dma_counts = {}
        self.base = {}

    def _collect(self, eng, reads, writes, nobase):
        w = {}

        def add(t):
            if t is None:
                return
            k, v = t
            if w.get(k, 0) < v:
                w[k] = v

        if not nobase:
            for k, v in self.base.items():
                add((k, v))
        for k in reads:
            add(self.last_w.get(k))
        for k in writes:
            add(self.last_w.get(k))
            for rk, rv in self.readers.get(k, {}).items():
                add((rk, rv))
        if eng == "pe":
            w.pop("pe", None)
        return w

    def _register(self, tok, reads, writes):
        for k in reads:
            d = self.readers.setdefault(k, {})
            if d.get(tok[0], 0) < tok[1]:
                d[tok[0]] = tok[1]
        for k in writes:
            self.last_w[k] = tok
            self.readers[k] = {}

    def op(self, eng, fns, reads=(), writes=(), nobase=False):
        if not isinstance(fns, (list, tuple)):
            fns = [fns]
        self.ticks[eng] += 1
        tok = (eng, self.ticks[eng])
        waits = self._collect(eng, reads, writes, nobase)
        self.ops[eng].append((list(fns), waits, tok, None))
        self._register(tok, reads, writes)
        return tok

    def dma(self, eng, fn, semkey, reads=(), writes=(), nobase=False):
        self.dma_counts[semkey] = self.dma_counts.get(semkey, 0) + 16
        tok = (semkey, self.dma_counts[semkey])
        waits = self._collect("dma:" + semkey, reads, writes, nobase)
        self.ops[eng].append(([fn], waits, None, semkey))
        self._register(tok, reads, writes)
        return tok

    def barrier(self):
        for e in ("pe", "act", "dve"):
            if self.ticks[e]:
                self.base[e] = self.ticks[e]
        for k, v in self.dma_counts.items():
            if not k.startswith("ring"):
                self.base[k] = v

    def final_wait(self, eng, toks):
        w = {}
        for k, v in toks:
            w[k] = max(w.get(k, 0), v)
        self.ops[eng].append(([], w, None, None))

    def check(self):
        pc = {e: 0 for e in self.ENGS}
        sem = {}
        progress = True
        while progress:
            progress = False
            for e in self.ENGS:
                while pc[e] < len(self.ops[e]):
                    fns, waits, tok, dmakey = self.ops[e][pc[e]]
                    if any(sem.get(k, 0) < v for k, v in waits.items()):
                        break
                    if tok is not None:
                        sem[tok[0]] = sem.get(tok[0], 0) + 1
                        assert sem[tok[0]] == tok[1], (tok, sem[tok[0]])
                    if dmakey is not None:
                        sem[dmakey] = sem.get(dmakey, 0) + 16
                    pc[e] += 1
                    progress = True
        stuck = {e: (pc[e], len(self.ops[e])) for e in self.ENGS if pc[e] < len(self.ops[e])}
        if stuck:
            for e, (p, n) in stuck.items():
                fns, waits, tok, dmakey = self.ops[e][p]
                print("STUCK", e, p, n, {k: (v, sem.get(k, 0)) for k, v in waits.items() if sem.get(k, 0) < v})
            raise RuntimeError("deadlock in semaphore protocol")

    def emit(self, block, sems):
        needed = {e: set() for e in self.ENGS}
        for e in self.ENGS:
            for fns, waits, tok, dmakey in self.ops[e]:
                for k, v in waits.items():
                    if k in needed:
                        needed[k].add(v)
        remap = {e: {v: i + 1 for i, v in enumerate(sorted(needed[e]))} for e in self.ENGS}

        def run(engname, engine):
            waited = {}
            for fns, waits, tok, dmakey in self.ops[engname]:
                for k, v in waits.items():
                    if k in remap:
                        v = remap[k][v]
                    if waited.get(k, 0) < v:
                        engine.wait_ge(sems[k], v)
                        waited[k] = v
                inst = None
                for fn in fns:
                    inst = fn(engine)
                if tok is not None and tok[1] in remap[tok[0]]:
                    inst.then_inc(sems[tok[0]], 1)
                if dmakey is not None:
                    inst.then_inc(sems[dmakey], 16)

        @block.tensor
        def _(e):
            run("pe", e)

        @block.scalar
        def _(e):
            run("act", e)

        @block.vector
        def _(e):
            run("dve", e)

        @block.gpsimd
        def _(e):
            run("pool", e)

        @block.sync
        def _(e):
            run("sp", e)


def MM(out, lhsT, rhs, start, stop):
    return lambda e: e.matmul(out, lhsT=lhsT, rhs=rhs, start=start, stop=stop)


def ACTV(out, in_, func, bias=None, scale=1.0, accum_out=None):
    kw = {}
    if bias is not None:
        kw["bias"] = bias
    if accum_out is not None:
        kw["accum_out"] = accum_out
    return lambda e: e.activation(out=out, in_=in_, func=func, scale=scale, **kw)


def TTOP(out, in0, in1, op):
    return lambda e: e.tensor_tensor(out=out, in0=in0, in1=in1, op=op)


def STT(out, in0, scalar, in1, op0, op1):
    return lambda e: e.scalar_tensor_tensor(out=out, in0=in0, scalar=scalar, in1=in1, op0=op0, op1=op1)


def TS(out, in0, s1, s2, op0, op1=None):
    if op1 is None:
        return lambda e: e.tensor_scalar(out=out, in0=in0, scalar1=s1, scalar2=s2, op0=op0)
    return lambda e: e.tensor_scalar(out=out, in0=in0, scalar1=s1, scalar2=s2, op0=op0, op1=op1)


def RECIP(out, in_):
    return lambda e: e.reciprocal(out=out, in_=in_)


def COPY(out, in_):
    return lambda e: e.tensor_copy(out=out, in_=in_)


def MEMSET(ap, v):
    return lambda e: e.memset(ap, v)


def DMA(out, in_):
    return lambda e: e.dma_start(out=out, in_=in_)


def blks(lo, w):
    return range(lo // 128, (lo + w + 127) // 128)


def _col_tile(W, cols):
    sub = W[:, cols]
    return np.ascontiguousarray(sub.reshape(16, 128, 128).transpose(1, 0, 2)).reshape(128, 2048)


def _down_tile(W, j0, Gg, d0, DD):
    sub = W[j0 * 128:(j0 + Gg) * 128, d0:d0 + DD]
    return np.ascontiguousarray(sub.reshape(Gg, 128, DD).transpose(1, 0, 2)).reshape(128, 2048)


def tile_plan():
    plan = []

    def ffn(tag):
        for (j0, Gg) in GROUPS:
            for fi in range(Gg):
                plan.append((tag + "_gate", "col", (j0 + fi) * 128))
                plan.append((tag + "_up", "col", (j0 + fi) * 128))
            DD = 2048 // Gg
            for dq in range(Gg):
                plan.append((tag + "_down", "down", (j0, Gg, dq * DD, DD)))

    ffn("w1")
    plan.append(("w_in", "cols", ("kk", 0)))
    plan.append(("w_in", "cols", ("kk", 1)))
    plan.append(("w_in", "cols", ("vv", 0)))
    plan.append(("w_in", "cols", ("vv", 1)))
    for j in range(8):
        plan.append(("w_in", "col", 2048 + j * 128))
    for j in range(8):
        plan.append(("w_in", "col", 1024 + j * 128))
    for j in range(8):
        plan.append(("w_in", "col", j * 128))
    for j in range(16):
        plan.append(("w_xkv", "col", j * 128))
    for j in range(16):
        plan.append(("w_xkv", "col", 2048 + j * 128))
    for j in range(16):
        plan.append(("w_out", "col", j * 128))
    for j in range(16):
        plan.append(("w_xq", "col", j * 128))
    for j in range(16):
        plan.append(("w_xo", "col", j * 128))
    ffn("w2")
    return plan


def build_stream(inp):
    plan = tile_plan()
    mats = {
        "w1_gate": inp["w1_gate"][0], "w1_up": inp["w1_up"][0], "w1_down": inp["w1_down"][0],
        "w2_gate": inp["w2_gate"][0], "w2_up": inp["w2_up"][0], "w2_down": inp["w2_down"][0],
        "w_in": inp["w_in"][0], "w_out": inp["w_out"][0], "w_xq": inp["w_xq"][0],
        "w_xkv": inp["w_xkv"][0], "w_xo": inp["w_xo"][0],
    }
    out = np.empty((len(plan), 128, 2048), dtype=np.float32)
    for i, (name, kind, arg) in enumerate(plan):
        W = mats[name]
        if kind == "col":
            out[i] = _col_tile(W, np.arange(arg, arg + 128))
        elif kind == "down":
            out[i] = _down_tile(W, *arg)
        else:
            what, kv = arg
            base = 3072 if what == "kk" else 3200
            c = np.arange(base + kv * 64, base + kv * 64 + 64)
            out[i] = _col_tile(W, np.concatenate([c, c]))
    return out


def build_nc(ntiles):
    nc = bass.Bass("TRN2", target_bir_lowering=False)
    xT_d = nc.dram_tensor("xT", [128, 16, TT], F32, kind="ExternalInput").ap()
    memT_d = nc.dram_tensor("memT", [128, 16, NMEM], F32, kind="ExternalInput").ap()
    ws_d = nc.dram_tensor("wstream", [ntiles, 128, 2048], F32, kind="ExternalInput").ap()
    cstA_d = nc.dram_tensor("cstA", [128, 128], F32, kind="ExternalInput").ap()
    bbc_d = nc.dram_tensor("bbc", [128, 1024], F32, kind="ExternalInput").ap()
    gvbc_d = nc.dram_tensor("gvbc", [128, 1024], F32, kind="ExternalInput").ap()
    stg_d = nc.dram_tensor("stg", [128, 2560], F32, kind="ExternalInput").ap()
    outT_d = nc.dram_tensor("outT", [128, 16, T], F32, kind="ExternalOutput").ap()
    dbg_d = None
    if DEBUG_PHASE:
        dbg_d = nc.dram_tensor("dbg", [128, 16, TT], F32, kind="ExternalOutput").ap()

    S = Sched()
    with ExitStack() as st:
        ARENA = 212000
        arena = st.enter_context(nc.sbuf_tensor("arena", [128, ARENA], U8))
        base_addr = None
        for a in nc.allocations:
            if getattr(a, "name", "") == "arena_set":
                base_addr = a.memorylocations[0].addr
        assert base_addr is not None

        def at(name, shape, dt, off):
            esz = 4 if dt == F32 else 2
            n = 1
            for s_ in shape[1:]:
                n *= s_
            assert off + n * esz <= ARENA, (name, off, n * esz)
            assert off % 32 == 0, (name, off)
            return nc.alloc_sbuf_tensor_at(name, shape, dt, offset=base_addr + off)

        h = at("h", [128, 16, TT], F32, 0)
        RING0 = 73728
        ring = at("ring", [128, NB, 2048], BF16, RING0)
        C0 = RING0 + NB * 4096
        cstA = at("cstA_s", [128, 128], F32, C0)
        esink = at("esink", [128, 16], F32, C0 + 512)
        epsc = at("epsc", [128, 8], F32, C0 + 576)
        ones = at("ones", [128, 128], BF16, C0 + 608)
        masks = at("masks", [128, 4, 128], BF16, C0 + 864)
        wsTm = at("wsTm", [128, 8, 128], BF16, C0 + 1888)
        bbc = at("bbc_s", [128, 8, 128], F32, C0 + 3936)
        gvbc = at("gvbc_s", [128, 1024], F32, C0 + 8032)
        ssv = at("ssv", [128, 8], F32, C0 + 12128)
        rstdv = at("rstdv", [128, 8], F32, C0 + 12160)
        SC = C0 + 12800
        SCSZ = ARENA - SC
        NS = SC + 96256
        rstd = at("rstd", [128, TT], F32, NS)
        sq = at("sq", [128, 16, 256], BF16, NS + 4608)
        stg = at("stg_s", [128, 2560], F32, NS)
        sqB = at("sqB", [128, 16, 256], BF16, SC + 59392)
        xn = at("xn", [128, 16, TT], BF16, SC)
        act = at("act", [128, 8, TT], BF16, SC + 36864)
        stmp = at("stmp", [128, 2, 512], F32, SC + 55296)
        uT = at("uT", [128, 8, T], BF16, SC + 36864)
        vtok = at("vtok", [128, 8, 1024], BF16, SC + 53248)
        qT = at("qT", [128, 8, T], BF16, SC + 69632)
        kT2 = at("kT2", [128, 2, TT], BF16, SC + 86016)
        kO = at("kO", [128, 2, TT], BF16, NS)
        rstd_m = at("rstd_m", [128, 256], F32, SC + 61440)
        rd2 = at("rd2", [128, 2, 512], F32, SC + 62464)
        vtok2 = at("vtok2", [128, 9, 2, 128], BF16, SC + 90624)
        ycg = at("ycg", [128, 16, T], BF16, SC)
        PT = at("PT", [128, 2, 2, 512], BF16, SC + 32768)
        t1 = at("t1", [128, 2, 4, 128], F32, SC + 32768)
        rd = at("rd", [128, 512], F32, SC + 103424)
        sq2 = at("sq2", [128, 2, 512], BF16, SC + 103424)
        sqb = at("sqb", [128, 2, 2, 128], BF16, SC + 105472)
        ybf = at("ybf", [128, 2, 2, 128], F32, SC + 106496)
        junk = at("junk", [128, 1024], BF16, SC + 106496)
        wtmp = at("wtmp", [128, 2, 512], F32, SC + 69632)
        memT = at("memT_s", [128, 16, NMEM], F32, SC + 36864)
        memn = at("memn", [128, 16, NMEM], BF16, SC + 53248)
        kxT = at("kxT", [128, 16, NMEM], BF16, SC + 36864)
        vx = at("vx", [128, 2, 2048], BF16, SC + 45056)
        xn2 = at("xn2", [128, 16, T], BF16, SC)
        oxT = at("oxT", [128, 16, T], BF16, SC)
        PTx = at("PTx", [128, 2, 2, 512], BF16, SC + 32768)
        rdx = at("rdx", [128, 2, 512], F32, SC + 53248)
        qxT = at("qxT", [128, 16, T], BF16, SC + 57344)
        ostg = at("ostg", [128, 2, 4, T], F32, SC)
        sqF = at("sqF", [128, 16, T], BF16, SC + 36864)

        pb = [st.enter_context(nc.psum_tensor(f"pb{i}", [128, 512], F32)) for i in range(8)]

        semnames = list(Sched.ENGS) + [f"ring{i}" for i in range(NB)] + [
            "ld_x0", "ld_x1", "ld_x2", "ld_x3", "ld_x4", "ld_cA", "ld_bbc", "ld_gv", "ld_stg", "ld_mem", "st_out0", "st_out1", "st_dbg"]
        sems = {k: st.enter_context(nc.semaphore(k)) for k in semnames}
        block = st.enter_context(nc.Block())

        tile_idx = [0]

        def next_tile():
            i = tile_idx[0]
            tile_idx[0] += 1
            s = i % NB
            S.dma("pool", DMA(ring[:, s, :], ws_d[i]), f"ring{s}", writes=[("ring", s)], nobase=True)
            return s

        bank_rr = {"gu": 0, "dn": 0, "gen": 0}

        def hkeys(c, lo, w):
            return [("h", c, b) for b in blks(lo, w)]

        def rstd_from_bank(bank, w, dst_ap, Dn, wkeys, view4=False):
            src_ap = pb[bank][:, 0:w]
            if view4:
                src_ap = src_ap.rearrange("p (a b) -> p a b", a=4)
            S.op("act", ACTV(dst_ap, src_ap, AF.Ln, bias=epsc[:, 0:1], scale=1.0 / Dn),
                 reads=[("bank", bank), ("eps",)], writes=wkeys)
            S.op("act", ACTV(dst_ap, dst_ap, AF.Exp, scale=-0.5), reads=wkeys, writes=wkeys)

        norm_rr = [0]

        def norm_h(gcol0, lo, hi, dst, dst_name, dst_off):
            pieces = []
            p = lo
            while p < hi:
                w = min(256, hi - p)
                pieces.append((p, w))
                p += w
            sqbuf = [sq, sqB]

            def head(i):
                p, w = pieces[i]
                sb = sqbuf[i % 2]
                bk = 6 + i % 2
                S.op("act", ACTV(sb[:, :, 0:w], h[:, :, p:p + w], AF.Square),
                     reads=[k for c in range(16) for k in hkeys(c, p, w)], writes=[("sq", i % 2)])
                S.op("pe", [MM(pb[bk][:, 0:w], ones[:, :], sb[:, c, 0:w], c == 0, c == 15) for c in range(16)],
                     reads=[("sq", i % 2), ("ones",)], writes=[("bank", bk)])

            def tail(i):
                p, w = pieces[i]
                bk = 6 + i % 2
                rstd_from_bank(bk, w, rstd[:, p:p + w], D, [("rstd", b) for b in blks(p, w)])
                for c in range(16):
                    S.op("dve", STT(dst[:, c, p - dst_off:p - dst_off + w], h[:, c, p:p + w],
                                    cstA[:, gcol0 + c:gcol0 + c + 1], rstd[:, p:p + w], ALU.mult, ALU.mult),
                         reads=hkeys(c, p, w) + [("rstd", b) for b in blks(p, w)] + [("cstA",)],
                         writes=[(dst_name, c, b) for b in blks(p - dst_off, w)])

            for i in range(len(pieces)):
                head(i)
                if i >= 1:
                    tail(i - 1)
            tail(len(pieces) - 1)

        def run_pipeline(n, stages, lags):
            for step in range(n + max(lags)):
                for fn, lag in zip(stages, lags):
                    i = step - lag
                    if 0 <= i < n:
                        fn(i)


        def ffn(pieces):
            for (j0, Gg) in GROUPS:
                for fi in range(Gg):
                    sg = next_tile()
                    su = next_tile()
                    for (lo, w) in pieces:
                        k = bank_rr["gu"] % 2
                        bank_rr["gu"] += 1
                        bA, bB = 2 * k, 2 * k + 1
                        xr = [("xn", c, b) for c in range(16) for b in blks(lo, w)]
                        S.op("pe", [MM(pb[bA][:, 0:w], ring[:, sg, c * 128:(c + 1) * 128], xn[:, c, lo:lo + w],
                                       c == 0, c == 15) for c in range(16)],
                             reads=[("ring", sg)] + xr, writes=[("bank", bA)])
                        S.op("pe", [MM(pb[bB][:, 0:w], ring[:, su, c * 128:(c + 1) * 128], xn[:, c, lo:lo + w],
                                       c == 0, c == 15) for c in range(16)],
                             reads=[("ring", su)] + xr, writes=[("bank", bB)])
                        S.op("act", ACTV(stmp[:, k, 0:w], pb[bA][:, 0:w], AF.Silu),
                             reads=[("bank", bA)], writes=[("stmp", k)])
                        S.op("dve", TTOP(act[:, fi, lo:lo + w], stmp[:, k, 0:w], pb[bB][:, 0:w], ALU.mult),
                             reads=[("stmp", k), ("bank", bB)], writes=[("act", fi, b) for b in blks(lo, w)])
                DD = 2048 // Gg
                for dq in range(Gg):
                    sd = next_tile()
                    for dc in range(DD // 128):
                        d = dq * (DD // 128) + dc
                        for (lo, w) in pieces:
                            bC = 4 + bank_rr["dn"] % 2
                            bank_rr["dn"] += 1
                            S.op("pe", [MM(pb[bC][:, 0:w], ring[:, sd, fi * DD + dc * 128:fi * DD + dc * 128 + 128],
                                           act[:, fi, lo:lo + w], fi == 0, fi == Gg - 1) for fi in range(Gg)],
                                 reads=[("ring", sd)] + [("act", fi, b) for fi in range(Gg) for b in blks(lo, w)],
                                 writes=[("bank", bC)])
                            S.op("dve", STT(h[:, d, lo:lo + w], pb[bC][:, 0:w], 0.5, h[:, d, lo:lo + w],
                                            ALU.mult, ALU.add),
                                 reads=[("bank", bC)] + hkeys(d, lo, w), writes=hkeys(d, lo, w))

        def dump_h_and_stop():
            tk = S.dma("sp", DMA(dbg_d[:], h[:]), "st_dbg",
                       reads=[("h", c, b) for c in range(16) for b in range(9)])
            return tk

        S.dma("sp", DMA(cstA[:], cstA_d[:]), "ld_cA", writes=[("cstA",)])
        S.dma("sp", DMA(stg[:], stg_d[:]), "ld_stg", writes=[("stg",)])
        S.op("dve", MEMSET(epsc[:], EPS), writes=[("eps",)])
        S.op("dve", MEMSET(ones[:], 1.0), writes=[("ones",)])
        S.op("dve", MEMSET(ssv[:], 0.0), writes=[("ssv",)])
        S.op("dve", COPY(masks[:].rearrange("p a b -> p (a b)"), stg[:, 0:512]), reads=[("stg",)], writes=[("masks",)])
        S.op("dve", TTOP(wsTm[:].rearrange("p a b -> p (a b)"), stg[:, 512:1536], stg[:, 1536:2560], ALU.mult),
             reads=[("stg",)], writes=[("wsTm",)])
        S.op("act", ACTV(esink[:], cstA[:, C_SINK:C_SINK + 16], AF.Exp), reads=[("cstA",)], writes=[("esink",)])
        S.barrier()
        for q, (lo_, w_) in enumerate([(0, 256), (256, 256), (512, 256), (768, 256), (1024, 128)]):
            S.dma("sp", DMA(h[:, :, lo_:lo_ + w_], xT_d[:, :, lo_:lo_ + w_]), f"ld_x{q}",
                  writes=[("h", c, b) for c in range(16) for b in blks(lo_, w_)], nobase=True)
        S.dma("sp", DMA(bbc[:].rearrange("p a b -> p (a b)"), bbc_d[:]), "ld_bbc", writes=[("bbc",)], nobase=True)
        S.dma("sp", DMA(gvbc[:], gvbc_d[:]), "ld_gv", writes=[("gvbc",)], nobase=True)

        norm_h(G_FFN1, 0, TT, xn, "xn", 0)
        ffn([(0, 384), (384, 384), (768, 384)])
        if DEBUG_PHASE == 1:
            tk = dump_h_and_stop()
            S.final_wait("sp", [tk])
            S.emit(block, sems)
            return nc

        S.barrier()
        norm_h(G_MIX, 0, TT, xn, "xn", 0)
        S.barrier()
        own = [(128, 512), (640, 512)]
        all3 = [(0, 384), (384, 384), (768, 384)]
        evac_rr = [0]

        def evac_copy(out, in_, reads, writes):
            evac_rr[0] += 1
            if evac_rr[0] % 2:
                S.op("act", ACTV(out, in_, AF.Copy), reads=reads, writes=writes)
            else:
                S.op("dve", COPY(out, in_), reads=reads, writes=writes)

        def gen_bank():
            b = bank_rr["gen"] % 4
            bank_rr["gen"] += 1
            return b

        def fm_tile(s, pieces, evac):
            for (lo, w) in pieces:
                b = gen_bank()
                S.op("pe", [MM(pb[b][:, 0:w], ring[:, s, c * 128:(c + 1) * 128], xn[:, c, lo:lo + w], c == 0, c == 15)
                            for c in range(16)],
                     reads=[("ring", s)] + [("xn", c, bb) for c in range(16) for bb in blks(lo, w)],
                     writes=[("bank", b)])
                evac(b, lo, w)

        def tm_tile(s, blocks, evac):
            for i0 in range(0, len(blocks), 4):
                grp = blocks[i0:i0 + 4]
                b = gen_bank()
                for qi, blk in enumerate(grp):
                    S.op("pe", [MM(pb[b][:, qi * 128:(qi + 1) * 128], xn[:, c, blk * 128:(blk + 1) * 128],
                                   ring[:, s, c * 128:(c + 1) * 128], c == 0, c == 15) for c in range(16)],
                         reads=[("ring", s)] + [("xn", c, blk) for c in range(16)], writes=[("bank", b)])
                evac(b, grp)

        S.op("dve", MEMSET(kT2[64:128, :, :], 0.0), writes=[("kT2z",)])
        S.op("dve", MEMSET(kO[0:64, :, :], 0.0), writes=[("kOz",)])

        def k_evac(b, lo, w, kv):
            S.op("act", ACTV(kT2[0:64, kv, lo:lo + w], pb[b][0:64, 0:w], AF.Copy),
                 reads=[("bank", b)], writes=[("kT2", kv, bb) for bb in blks(lo, w)])
            S.op("dve", COPY(kO[64:128, kv, lo:lo + w], pb[b][64:128, 0:w]),
                 reads=[("bank", b)], writes=[("kO", kv, bb) for bb in blks(lo, w)])

        for kv in range(2):
            s = next_tile()
            fm_tile(s, all3, lambda b, lo, w, kv=kv: k_evac(b, lo, w, kv))
        for kv in range(2):
            s = next_tile()
            tm_tile(s, list(range(9)), lambda b, grp, kv=kv: evac_copy(
                vtok2[:, grp[0]:grp[0] + len(grp), kv, :],
                pb[b][:, 0:128 * len(grp)].rearrange("p (a b) -> p a b", a=len(grp)),
                [("bank", b)], [("vtok2", blk, kv) for blk in grp]))
        for j in range(8):
            s = next_tile()
            fm_tile(s, own, lambda b, lo, w, j=j: evac_copy(
                qT[:, j, lo - 128:lo - 128 + w], pb[b][:, 0:w], [("bank", b)],
                [("qT", j, bb) for bb in blks(lo - 128, w)]))
        for j in range(8):
            s = next_tile()
            tm_tile(s, list(range(1, 9)), lambda b, grp, j=j: S.op(
                "act", ACTV(vtok[:, grp[0] - 1:grp[0] - 1 + len(grp), j * 128:(j + 1) * 128],
                            pb[b][:, 0:128 * len(grp)].rearrange("p (a b) -> p a b", a=len(grp)), AF.Gelu),
                reads=[("bank", b)], writes=[("vtok", blk - 1) for blk in grp]))
        for blk in range(8):
            S.op("act", ACTV(junk[:], vtok[:, blk, :], AF.Square, accum_out=ssv[:, blk:blk + 1]),
                 reads=[("vtok", blk), ("ssv",)], writes=[("junk",), ("ssvb", blk)])
        S.op("act", ACTV(rstdv[:], ssv[:], AF.Ln, bias=epsc[:, 0:1], scale=1.0 / 1024),
             reads=[("ssvb", b) for b in range(8)] + [("eps",)], writes=[("rstdv",)])
        S.op("act", ACTV(rstdv[:], rstdv[:], AF.Exp, scale=-0.5), reads=[("rstdv",)], writes=[("rstdv",)])
        for blk in range(8):
            S.op("dve", STT(vtok[:, blk, :], vtok[:, blk, :], rstdv[:, blk:blk + 1], gvbc[:, :], ALU.mult, ALU.mult),
                 reads=[("vtok", blk), ("rstdv",), ("gvbc",), ("junk",)], writes=[("vtok", blk)])
        for j in range(8):
            s = next_tile()
            fm_tile(s, own, lambda b, lo, w, j=j: S.op(
                "act", ACTV(uT[:, j, lo - 128:lo - 128 + w], pb[b][:, 0:w], AF.Gelu),
                reads=[("bank", b)], writes=[("uT", j, bb) for bb in blks(lo - 128, w)]))

        S.barrier()
        sg_iters = [(blk, half) for blk in range(8) for half in range(2)]
        sg_bank = {}

        def sg_A(i):
            blk, half = sg_iters[i]
            g0 = 4 * half
            b = i % 4
            sg_bank[i] = b
            S.op("pe", [MM(pb[b][:, gg * 128:(gg + 1) * 128], vtok[:, blk, (g0 + gg) * 128:(g0 + gg + 1) * 128],
                           wsTm[:, g0 + gg, :], True, True) for gg in range(4)],
                 reads=[("vtok", blk), ("wsTm",)], writes=[("bank", b)])

        def sg_B(i):
            blk, half = sg_iters[i]
            g0 = 4 * half
            kk = i % 2
            b = sg_bank[i]
            S.op("dve", TTOP(t1[:, kk, :, :], pb[b][:, :].rearrange("p (a b) -> p a b", a=4),
                             bbc[:, g0:g0 + 4, :], ALU.add),
                 reads=[("bank", b), ("bbc",)], writes=[("t1", kk)])
            S.op("dve", TTOP(t1[:, kk, :, :], t1[:, kk, :, :], uT[:, g0:g0 + 4, blk * 128:(blk + 1) * 128], ALU.mult),
                 reads=[("t1", kk)] + [("uT", g0 + gg, blk) for gg in range(4)], writes=[("t1", kk)])
            S.op("dve", TTOP(ycg[:, g0:g0 + 4, blk * 128:(blk + 1) * 128], t1[:, kk, :, :],
                             cstA[:, G_AOUT + g0:G_AOUT + g0 + 4].unsqueeze(2).to_broadcast([128, 4, 128]), ALU.mult),
                 reads=[("t1", kk), ("cstA",)], writes=[("ycg", g0 + gg, blk) for gg in range(4)])
            S.op("act", ACTV(sq2[:, kk, :], t1[:, kk, :, :].rearrange("p a b -> p (a b)"), AF.Square),
                 reads=[("t1", kk)], writes=[("sq2", kk)])

        def sg_C(i):
            blk, half = sg_iters[i]
            kk = i % 2
            bss = 4 + blk // 4
            S.op("pe", [MM(pb[bss][:, (blk % 4) * 128:(blk % 4 + 1) * 128], ones[:, :],
                           sq2[:, kk, gg * 128:(gg + 1) * 128],
                           half == 0 and gg == 0, half == 1 and gg == 3) for gg in range(4)],
                 reads=[("sq2", kk), ("ones",)], writes=[("bank", bss)])

        run_pipeline(16, [sg_C, sg_A, sg_B], [3, 0, 1])
        for half in range(2):
            rstd_from_bank(4 + half, 512, h[:, 4 * half:4 * half + 4, 0:128], 1024,
                           [("h", 4 * half + q, 0) for q in range(4)], view4=True)

        S.barrier()
        S.dma("sp", DMA(memT[:], memT_d[:]), "ld_mem", writes=[("memT",)])
        S.op("act", ACTV(sq[:, :, 0:256], memT[:, :, :], AF.Square), reads=[("memT",)], writes=[("sq", 0)])
        S.op("pe", [MM(pb[6][:, 0:256], ones[:, :], sq[:, c, 0:256], c == 0, c == 15) for c in range(16)],
             reads=[("sq", 0), ("ones",)], writes=[("bank", 6)])
        rstd_from_bank(6, 256, rstd_m[:, :], D, [("rstd_m",)])
        for c in range(16):
            S.op("dve", STT(memn[:, c, :], memT[:, c, :], cstA[:, G_MEM + c:G_MEM + c + 1], rstd_m[:, :],
                            ALU.mult, ALU.mult),
                 reads=[("memT",), ("rstd_m",), ("cstA",)], writes=[("memn", c)])
        S.barrier()

        def kvx_tile(i):
            s = next_tile()
            if i < 16:
                j = i
                S.op("pe", [MM(pb[7][:, 0:256], ring[:, s, c * 128:(c + 1) * 128], memn[:, c, :], c == 0, c == 15)
                            for c in range(16)],
                     reads=[("ring", s)] + [("memn", c) for c in range(16)], writes=[("bank", 7)])
                evac_copy(kxT[:, j, :], pb[7][:, 0:256], [("bank", 7)], [("kxT", j)])
            else:
                j = i - 16
                for mb in range(2):
                    S.op("pe", [MM(pb[7][:, mb * 128:(mb + 1) * 128], memn[:, c, mb * 128:(mb + 1) * 128],
                                   ring[:, s, c * 128:(c + 1) * 128], c == 0, c == 15) for c in range(16)],
                         reads=[("ring", s)] + [("memn", c) for c in range(16)], writes=[("bank", 7)])
                evac_copy(vx[:, :, j * 128:(j + 1) * 128], pb[7][:, 0:256].rearrange("p (a b) -> p a b", a=2),
                          [("bank", 7)], [("vx", j)])

        att_iters = [(n, hg) for n in range(8) for hg in range(4)]

        def att_A1(i):
            n, hg = att_iters[i]
            kvh = hg // 2
            kk = i % 2
            for kb in range(2):
                bS = kb
                kblk = n + kb
                fns = []
                for hh in range(4):
                    hd = 4 * hg + hh
                    j, half = hd // 2, hd % 2
                    ksrc = kT2 if half == 0 else kO
                    fns.append(MM(pb[bS][:, hh * 128:(hh + 1) * 128],
                                  ksrc[:, kvh, kblk * 128:(kblk + 1) * 128],
                                  qT[:, j, n * 128:(n + 1) * 128], True, True))
                S.op("pe", fns,
                     reads=[("kT2", kvh, kblk), ("kO", kvh, kblk), ("kT2z",), ("kOz",)] +
                           [("qT", (4 * hg + hh) // 2, n) for hh in range(4)],
                     writes=[("bank", bS)])
                S.op("act", ACTV(PT[:, kk, kb, :], pb[bS][:, :], AF.Exp, scale=0.125),
                     reads=[("bank", bS)], writes=[("PT", kk, kb)])

        def att_A2(i):
            n, hg = att_iters[i]
            kk = i % 2
            m0 = 2 if n == 0 else 0
            S.op("dve", TTOP(PT[:, kk, :, :].rearrange("p k (h q) -> p k h q", h=4),
                             PT[:, kk, :, :].rearrange("p k (h q) -> p k h q", h=4),
                             masks[:, m0:m0 + 2, :].unsqueeze(2).to_broadcast([128, 2, 4, 128]), ALU.mult),
                 reads=[("PT", kk, 0), ("PT", kk, 1), ("masks",)], writes=[("PT", kk, 0), ("PT", kk, 1)])

        def att_B1(i):
            n, hg = att_iters[i]
            kvh = hg // 2
            kk = i % 2
            bD, bO = 2 + 2 * kk, 3 + 2 * kk
            S.op("pe", [MM(pb[bD][:, :], ones[:, :], PT[:, kk, 0, :], True, False),
                        MM(pb[bD][:, :], ones[:, :], PT[:, kk, 1, :], False, True)],
                 reads=[("PT", kk, 0), ("PT", kk, 1), ("ones",)], writes=[("bank", bD)])
            S.op("pe", [MM(pb[bO][:, :], vtok2[:, n, kvh, :], PT[:, kk, 0, :], True, False),
                        MM(pb[bO][:, :], vtok2[:, n + 1, kvh, :], PT[:, kk, 1, :], False, True)],
                 reads=[("PT", kk, 0), ("PT", kk, 1), ("vtok2", n, kvh), ("vtok2", n + 1, kvh)],
                 writes=[("bank", bO)])
            S.op("dve", TTOP(rd2[:, kk, :].rearrange("p (a b) -> p a b", a=4),
                             pb[bD][:, :].rearrange("p (a b) -> p a b", a=4),
                             esink[:, 4 * hg:4 * hg + 4].unsqueeze(2).to_broadcast([128, 4, 128]), ALU.add),
                 reads=[("bank", bD), ("esink",)], writes=[("rd", kk)])

        def att_B2(i):
            kk = i % 2
            S.op("act", ACTV(rd2[:, kk, :], rd2[:, kk, :], AF.Ln), reads=[("rd", kk)], writes=[("rd", kk)])
            S.op("act", ACTV(rd2[:, kk, :], rd2[:, kk, :], AF.Exp, scale=-1.0), reads=[("rd", kk)],
                 writes=[("rd", kk)])

        def att_B3(i):
            n, hg = att_iters[i]
            kk = i % 2
            bO = 3 + 2 * kk
            for half in range(2):
                p0 = half * 64
                S.op("dve", TTOP(ybf[p0:p0 + 64, kk, :, :],
                                 pb[bO][p0:p0 + 64, :].rearrange("p (a b c) -> p a b c", a=2, b=2)[:, :, half, :],
                                 rd2[p0:p0 + 64, kk, :].rearrange("p (a b c) -> p a b c", a=2, b=2)[:, :, half, :],
                                 ALU.mult),
                     reads=[("bank", bO), ("rd", kk)], writes=[("ybf", kk, half)])
            S.op("dve", TTOP(ycg[:, 8 + 2 * hg:8 + 2 * hg + 2, n * 128:(n + 1) * 128], ybf[:, kk, :, :],
                             cstA[:, G_BOUT + 2 * hg:G_BOUT + 2 * hg + 2].unsqueeze(2).to_broadcast([128, 2, 128]),
                             ALU.mult),
                 reads=[("ybf", kk, 0), ("ybf", kk, 1), ("cstA",)],
                 writes=[("ycg", 8 + 2 * hg, n), ("ycg", 8 + 2 * hg + 1, n)])

        def att_B4(i):
            kk = i % 2
            S.op("act", ACTV(sqb[:, kk, :, :], ybf[:, kk, :, :], AF.Square),
                 reads=[("ybf", kk, 0), ("ybf", kk, 1)], writes=[("sqb", kk)])

        def att_C(i):
            n, hg = att_iters[i]
            kk = i % 2
            bss = 6
            S.op("pe", [MM(pb[bss][:, (n % 4) * 128:(n % 4 + 1) * 128], ones[:, :], sqb[:, kk, i2, :],
                           hg == 0 and i2 == 0, hg == 3 and i2 == 1) for i2 in range(2)],
                 reads=[("sqb", kk), ("ones",)], writes=[("bank", bss)])
            if i % 16 == 15:
                hb = i // 16
                rstd_from_bank(6, 512, h[:, 8 + 4 * hb:8 + 4 * hb + 4, 0:128], 1024,
                               [("h", 8 + 4 * hb + q, 0) for q in range(4)], view4=True)

        run_pipeline(32, [att_C, att_B1, att_A1, att_B3, att_B2, att_A2, att_B4, kvx_tile], [4, 1, 0, 2, 1, 0, 2, 0])

        S.barrier()
        for j in range(16):
            s = next_tile()
            for pi, (lo, w) in enumerate(own):
                lo0 = lo - 128
                bA, bB = gen_bank(), gen_bank()
                for part, bnk in ((0, bA), (1, bB)):
                    S.op("pe", [MM(pb[bnk][:, 0:w], ring[:, s, (8 * part + c) * 128:(8 * part + c + 1) * 128],
                                   ycg[:, 8 * part + c, lo0:lo0 + w], c == 0, c == 7) for c in range(8)],
                         reads=[("ring", s)] + [("ycg", 8 * part + c, bb) for c in range(8) for bb in blks(lo0, w)],
                         writes=[("bank", bnk)])
                b0 = lo0 // 128
                S.op("dve", TTOP(wtmp[:, 0, :].rearrange("p (a b) -> p a b", a=4),
                                 pb[bA][:, :].rearrange("p (a b) -> p a b", a=4), h[:, b0:b0 + 4, 0:128], ALU.mult),
                     reads=[("bank", bA)] + [("h", b0 + q, 0) for q in range(4)], writes=[("wtmp", 0)])
                S.op("dve", TTOP(wtmp[:, 1, :].rearrange("p (a b) -> p a b", a=4),
                                 pb[bB][:, :].rearrange("p (a b) -> p a b", a=4), h[:, 8 + b0:8 + b0 + 4, 0:128],
                                 ALU.mult),
                     reads=[("bank", bB)] + [("h", 8 + b0 + q, 0) for q in range(4)], writes=[("wtmp", 1)])
                S.op("dve", TTOP(wtmp[:, 0, :], wtmp[:, 0, :], wtmp[:, 1, :], ALU.add),
                     reads=[("wtmp", 0), ("wtmp", 1)], writes=[("wtmp", 0)])
                S.op("dve", TTOP(h[:, j, lo:lo + w], h[:, j, lo:lo + w], wtmp[:, 0, :], ALU.add),
                     reads=[("wtmp", 0)] + hkeys(j, lo, w), writes=hkeys(j, lo, w))
        if DEBUG_PHASE == 2:
            tk = dump_h_and_stop()
            S.final_wait("sp", [tk])
            S.emit(block, sems)
            return nc

        S.barrier()
        for c in range(16):
            S.op("dve", TS(xn2[:, c, :], h[:, c, 128:TT], cstA[:, G_X + c:G_X + c + 1], None, ALU.mult),
                 reads=hkeys(c, 128, T) + [("cstA",)], writes=[("xn2", c, bb) for bb in range(8)])
        def x_stat(pi):
            p, w = 128 + 256 * pi, 256
            bk = 6 + pi % 2
            S.op("act", ACTV(sq[:, :, 0:w], h[:, :, p:p + w], AF.Square),
                 reads=[k for c in range(16) for k in hkeys(c, p, w)], writes=[("sq", 0)])
            S.op("pe", [MM(pb[bk][:, 0:w], ones[:, :], sq[:, c, 0:w], c == 0, c == 15) for c in range(16)],
                 reads=[("sq", 0), ("ones",)], writes=[("bank", bk)])
            rstd_from_bank(bk, w, rstd[:, p:p + w], D, [("rstd", b) for b in blks(p, w)])

        own0 = [(0, 512), (512, 512)]
        xq_slot = {}

        xq_bank = {}

        def xq_mm(j, pi):
            if j not in xq_slot:
                xq_slot[j] = next_tile()
            s = xq_slot[j]
            lo, w = own0[pi]
            b = gen_bank()
            xq_bank[(j, pi)] = b
            S.op("pe", [MM(pb[b][:, 0:w], ring[:, s, c * 128:(c + 1) * 128], xn2[:, c, lo:lo + w], c == 0, c == 15)
                        for c in range(16)],
                 reads=[("ring", s)] + [("xn2", c, bb) for c in range(16) for bb in blks(lo, w)],
                 writes=[("bank", b)])

        def xq_ev(j, pi):
            lo, w = own0[pi]
            b = xq_bank[(j, pi)]
            S.op("dve", TTOP(qxT[:, j, lo:lo + w], pb[b][:, 0:w], rstd[:, 128 + lo:128 + lo + w], ALU.mult),
                 reads=[("bank", b)] + [("rstd", bb) for bb in blks(128 + lo, w)],
                 writes=[("qxT", j, bb) for bb in blks(lo, w)])

        x_stat(0)
        xq_mm(0, 0)
        x_stat(1)
        xq_ev(0, 0)
        xq_mm(1, 0)
        x_stat(2)
        xq_ev(1, 0)
        xq_mm(0, 1)
        x_stat(3)
        xq_ev(0, 1)
        xq_mm(1, 1)
        xq_ev(1, 1)
        for j in range(2, 16):
            for pi in range(2):
                xq_mm(j, pi)
                xq_ev(j, pi)
        S.barrier()
        xscale = float(512 ** -0.5)
        x_iters = [(hx, lo, w) for hx in range(4) for (lo, w) in own0]
        obank = [0]

        def x_A(i):
            hx, lo, w = x_iters[i]
            kk = i % 2
            for mb in range(2):
                b = 2 * kk + mb
                S.op("pe", [MM(pb[b][:, :], kxT[:, hx * 4 + dc, mb * 128:(mb + 1) * 128],
                               qxT[:, hx * 4 + dc, lo:lo + w], dc == 0, dc == 3) for dc in range(4)],
                     reads=[("kxT", hx * 4 + dc) for dc in range(4)] +
                           [("qxT", hx * 4 + dc, bb) for dc in range(4) for bb in blks(lo, w)],
                     writes=[("bank", b)])
                S.op("act", ACTV(PTx[:, kk, mb, :], pb[b][:, :], AF.Exp, scale=xscale),
                     reads=[("bank", b)], writes=[("PTx", kk, mb)])

        def x_B1(i):
            hx, lo, w = x_iters[i]
            kk = i % 2
            S.op("pe", [MM(pb[4][:, :], ones[:, :], PTx[:, kk, 0, :], True, False),
                        MM(pb[4][:, :], ones[:, :], PTx[:, kk, 1, :], False, True)],
                 reads=[("PTx", kk, 0), ("PTx", kk, 1), ("ones",)], writes=[("bank", 4)])
            S.op("act", ACTV(rdx[:, kk, :], pb[4][:, :], AF.Ln), reads=[("bank", 4)], writes=[("rdx", kk)])
            S.op("act", ACTV(rdx[:, kk, :], rdx[:, kk, :], AF.Exp, scale=-1.0), reads=[("rdx", kk)],
                 writes=[("rdx", kk)])

        def x_B2(i):
            hx, lo, w = x_iters[i]
            kk = i % 2
            for dc in range(4):
                bo = 5 + obank[0] % 3
                obank[0] += 1
                S.op("pe", [MM(pb[bo][:, :], vx[:, 0, (hx * 4 + dc) * 128:(hx * 4 + dc + 1) * 128],
                               PTx[:, kk, 0, :], True, False),
                            MM(pb[bo][:, :], vx[:, 1, (hx * 4 + dc) * 128:(hx * 4 + dc + 1) * 128],
                               PTx[:, kk, 1, :], False, True)],
                     reads=[("PTx", kk, 0), ("PTx", kk, 1), ("vx", hx * 4 + dc)], writes=[("bank", bo)])
                S.op("dve", TTOP(oxT[:, hx * 4 + dc, lo:lo + w], pb[bo][:, :], rdx[:, kk, :], ALU.mult),
                     reads=[("bank", bo), ("rdx", kk)], writes=[("oxT", hx * 4 + dc, bb) for bb in blks(lo, w)])

        run_pipeline(8, [x_B2, x_B1, x_A], [2, 1, 0])

        for j in range(16):
            s = next_tile()
            for (lo, w) in own0:
                b = gen_bank()
                S.op("pe", [MM(pb[b][:, 0:w], ring[:, s, c * 128:(c + 1) * 128], oxT[:, c, lo:lo + w], c == 0, c == 15)
                            for c in range(16)],
                     reads=[("ring", s)] + [("oxT", c, bb) for c in range(16) for bb in blks(lo, w)],
                     writes=[("bank", b)])
                S.op("dve", TTOP(h[:, j, 128 + lo:128 + lo + w], h[:, j, 128 + lo:128 + lo + w], pb[b][:, 0:w], ALU.add),
                     reads=[("bank", b)] + hkeys(j, 128 + lo, w), writes=hkeys(j, 128 + lo, w))
        if DEBUG_PHASE == 3:
            if os.environ.get("MK_VERBOSE"):
                print("ticks@3", S.ticks)
            tk = dump_h_and_stop()
            S.final_wait("sp", [tk])
            S.emit(block, sems)
            return nc

        S.barrier()
        norm_h(G_FFN2, 128, TT, xn, "xn", 0)
        ffn(own)

        if DEBUG_PHASE == 4:
            tk = dump_h_and_stop()
            S.final_wait("sp", [tk])
            S.emit(block, sems)
            return nc

        S.barrier()
        toks = []
        for blk4 in range(4):
            for c in range(4 * blk4, 4 * blk4 + 4):
                if c % 2 == 0:
                    S.op("act", ACTV(sqF[:, c, :], h[:, c, 128:TT], AF.Square), reads=hkeys(c, 128, T),
                         writes=[("sqF", c)])
                else:
                    S.op("dve", TTOP(sqF[:, c, :], h[:, c, 128:TT], h[:, c, 128:TT], ALU.mult),
                         reads=hkeys(c, 128, T), writes=[("sqF", c)])
            for half in range(2):
                S.op("pe", [MM(pb[6 + half][:, :], ones[:, :], sqF[:, c, half * 512:(half + 1) * 512], c == 0, c == 15)
                            for c in range(4 * blk4, 4 * blk4 + 4)],
                     reads=[("sqF", c) for c in range(4 * blk4, 4 * blk4 + 4)] + [("ones",)],
                     writes=[("bank", 6 + half)])
        for half in range(2):
            rstd_from_bank(6 + half, 512, rstd[:, 128 + half * 512:128 + (half + 1) * 512], D,
                           [("rstd", b) for b in blks(128 + half * 512, 512)])
        for q in range(4):
            kk = q % 2
            for ci in range(4):
                c = 4 * q + ci
                S.op("dve", STT(ostg[:, kk, ci, :], h[:, c, 128:TT], cstA[:, G_FIN + c:G_FIN + c + 1],
                                rstd[:, 128:TT], ALU.mult, ALU.mult),
                     reads=hkeys(c, 128, T) + [("rstd", b) for b in blks(128, T)] + [("cstA",)],
                     writes=[("ostg", kk, ci)])
            toks.append(S.dma("sp", DMA(outT_d[:, 4 * q:4 * q + 4, :], ostg[:, kk, :, :]), f"st_out{kk}",
                              reads=[("ostg", kk, ci) for ci in range(4)]))
        S.final_wait("sp", toks[-2:])
        if os.environ.get("MK_VERBOSE"):
            print("ticks", S.ticks, "dma", S.dma_counts)
            S.check()
        S.emit(block, sems)
    return nc


def _fm(v):
    return np.ascontiguousarray(np.asarray(v, np.float32).reshape(-1, 128).T)


def prep_inputs(inp):
    x = np.asarray(inp["x"], np.float32)[0]
    mem = np.asarray(inp["mem"], np.float32)[0]
    wstream = build_stream({k: np.asarray(v, np.float32) for k, v in inp.items()})
    xp = np.concatenate([np.zeros((HALO, D), np.float32), x], axis=0)
    memT = np.ascontiguousarray(mem.reshape(NMEM, 16, 128).transpose(2, 1, 0))
    cstA = np.zeros((128, 128), np.float32)
    cstA[:, G_FFN1:G_FFN1 + 16] = _fm(inp["g_ffn1"][0])
    cstA[:, G_MIX:G_MIX + 16] = _fm(inp["g_mix"][0])
    cstA[:, G_X:G_X + 16] = _fm(inp["g_x"][0])
    cstA[:, G_MEM:G_MEM + 16] = _fm(inp["g_mem"][0])
    cstA[:, G_FFN2:G_FFN2 + 16] = _fm(inp["g_ffn2"][0])
    cstA[:, G_FIN:G_FIN + 16] = _fm(inp["g_final"])
    cstA[:, G_AOUT:G_AOUT + 8] = _fm(inp["g_a_out"][0])
    cstA[:, G_BOUT:G_BOUT + 8] = _fm(inp["g_b_out"][0])
    cstA[:, C_SINK:C_SINK + 16] = np.broadcast_to(np.asarray(inp["sinks"], np.float32)[0][None, :], (128, 16))
    bbc = np.ascontiguousarray(np.broadcast_to(np.asarray(inp["b_s"], np.float32)[0].reshape(1, 1024), (128, 1024)))
    gvbc = np.ascontiguousarray(np.broadcast_to(np.asarray(inp["g_v"], np.float32)[0].reshape(1, 1024), (128, 1024)))
    si = np.arange(128)[:, None]
    qi = np.arange(128)[None, :]
    one, zero = np.float32(1), np.float32(0)
    maskC = np.where(si <= qi, one, zero).astype(np.float32)
    maskP = np.where(si > qi, one, zero).astype(np.float32)
    wsT = np.ascontiguousarray(np.asarray(inp["w_s"], np.float32)[0].transpose(2, 0, 1)).reshape(128, 1024)
    tril = np.where(si <= qi, np.float32(1), np.float32(0)).astype(np.float32)
    tril8 = np.ascontiguousarray(np.broadcast_to(tril[:, None, :], (128, 8, 128))).reshape(128, 1024)
    in_maps = []
    for i in range(NCORES):
        xs = xp[i * T:i * T + TT]
        xT = np.ascontiguousarray(xs.reshape(TT, 16, 128).transpose(2, 1, 0))
        maskH = maskP if i > 0 else np.zeros((128, 128), np.float32)
        stg = np.ascontiguousarray(np.concatenate([maskP, maskC, maskH, maskC, wsT, tril8], axis=1))
        in_maps.append({"xT": xT, "memT": memT, "wstream": wstream, "cstA": cstA, "bbc": bbc, "gvbc": gvbc,
                        "stg": stg})
    return in_maps, wstream.shape[0]


def kernel(**inputs):
    in_maps, ntiles = prep_inputs(inputs)
    nc = build_nc(ntiles)
    res = run_bass_kernel_spmd(nc, in_maps, core_ids=list(range(NCORES)))
    if DEBUG_PHASE:
        return res
    outs = []
    for r in res.results:
        oT = np.asarray(r["outT"])
        outs.append(oT.transpose(2, 1, 0).reshape(T, D))
    return np.concatenate(outs, axis=0).reshape(1, SEQ, D).astype(np.float32)
```
